# Optimizing a Trainium2 kernel written in Bass

```python
import math
import jax, jax.numpy as jnp
from jax import lax
import numpy as np

D_MODEL = 1024
BATCH = 32
SEQ = 2048
DEPTH = 2

N_MIXERS = 2
N_MOBA = (DEPTH + 1) // 2
N_GDN = DEPTH // 2

MOBA_HEAD_DIM = 128
MOBA_HEADS = D_MODEL // MOBA_HEAD_DIM
MOBA_WIDTH = MOBA_HEADS * MOBA_HEAD_DIM
MOBA_BLOCK = 256
MOBA_TOP_K = 3
MOBA_Q_CHUNK = 64
MOBA_IN = 4 * MOBA_WIDTH

GDN_HEAD_DIM = 128
GDN_HEADS = D_MODEL // GDN_HEAD_DIM
GDN_WIDTH = GDN_HEADS * GDN_HEAD_DIM
GDN_CONV = 4
GDN_CHUNK = 64
GDN_IN = 4 * GDN_WIDTH + 2 * GDN_HEADS

DEEPNORM_ALPHA = (2.0 * DEPTH) ** 0.25
DEEPNORM_BETA = (8.0 * DEPTH) ** -0.25
LN_EPS = 1e-5
RMS_EPS = 1e-6
L2_EPS = 1e-6
ADA_INIT_SCALE = 0.1
NEG = -1e30

kernel_name = "hybrid_moba_gdn_adaln_deepnorm"


def layer_norm(x, g, b):
    xf = x.astype(jnp.float32)
    mu = jnp.mean(xf, axis=-1, keepdims=True)
    var = jnp.mean(jnp.square(xf - mu), axis=-1, keepdims=True)
    y = (xf - mu) * lax.rsqrt(var + LN_EPS)
    return (y * g.astype(jnp.float32) + b.astype(jnp.float32)).astype(x.dtype)


def l2_normalize(x):
    return x * lax.rsqrt(jnp.sum(jnp.square(x), axis=-1, keepdims=True) + L2_EPS)


def causal_depthwise_conv(u, w):
    k_w, ch = w.shape
    return lax.conv_general_dilated(
        u, w[:, None, :], window_strides=(1,), padding=[(k_w - 1, 0)],
        dimension_numbers=('NWC', 'WIO', 'NWC'), feature_group_count=ch)


def moba_attention(q, k, v):
    B, S, H, Dh = q.shape
    L = MOBA_BLOCK
    nb = -(-S // L)
    s_pad = nb * L
    pad = [(0, 0), (0, s_pad - S), (0, 0), (0, 0)]
    qh = jnp.pad(q, pad).transpose(0, 2, 1, 3)
    kb = jnp.pad(k, pad).transpose(0, 2, 1, 3).reshape(B, H, nb, L, Dh)
    vb = jnp.pad(v, pad).transpose(0, 2, 1, 3).reshape(B, H, nb, L, Dh)

    k_mean = jnp.mean(kb.astype(jnp.float32), axis=3)
    gate = jnp.einsum('bhsd,bhnd->bhsn', qh.astype(jnp.float32), k_mean)
    q_blk = jnp.arange(s_pad) // L
    cand = jnp.arange(nb)[None, :] < q_blk[:, None]
    gate = jnp.where(cand, gate, NEG)
    n_sel = max(1, min(MOBA_TOP_K, nb - 1))
    _, sel = lax.top_k(gate, n_sel)
    sel_valid = sel < q_blk[:, None]

    qc_n = MOBA_Q_CHUNK
    nqc = s_pad // qc_n
    cpb = L // qc_n

    def to_items(a):
        rest = a.shape[3:]
        a = jnp.moveaxis(a.reshape(B, H, nqc, qc_n, *rest), 2, 1)
        return a.reshape(B * nqc, H, qc_n, *rest)

    b_ids = jnp.repeat(jnp.arange(B, dtype=jnp.int32), nqc)
    c_ids = jnp.tile(jnp.arange(nqc, dtype=jnp.int32), B)
    scale = MOBA_HEAD_DIM ** -0.5
    h_idx = jnp.arange(H)[:, None, None]

    def attend(item):
        qc, selc, validc, b, ci = item
        kbb = kb[b]
        vbb = vb[b]
        kg = kbb[h_idx, selc]
        vg = vbb[h_idx, selc]
        blk = ci // cpb
        k_own = lax.dynamic_index_in_dim(kbb, blk, axis=1, keepdims=False)
        v_own = lax.dynamic_index_in_dim(vbb, blk, axis=1, keepdims=False)
        s_sel = jnp.einsum('hqd,hqnld->hqnl', qc, kg).astype(jnp.float32) * scale
        s_sel = jnp.where(validc[..., None], s_sel, NEG)
        s_own = jnp.einsum('hqd,hld->hql', qc, k_own).astype(jnp.float32) * scale
        q_pos = (ci % cpb) * qc_n + jnp.arange(qc_n)
        causal = jnp.arange(L)[None, :] <= q_pos[:, None]
        s_own = jnp.where(causal, s_own, NEG)
        s_all = jnp.concatenate([s_sel.reshape(H, qc_n, n_sel * L), s_own], axis=-1)
        p = jax.nn.softmax(s_all, axis=-1).astype(v.dtype)
        p_sel = p[..., :n_sel * L].reshape(H, qc_n, n_sel, L)
        p_own = p[..., n_sel * L:]
        return (jnp.einsum('hqnl,hqnld->hqd', p_sel, vg)
                + jnp.einsum('hql,hld->hqd', p_own, v_own))

    out = lax.map(attend, (to_items(qh), to_items(sel), to_items(sel_valid), b_ids, c_ids))
    out = out.reshape(B, nqc, H, qc_n, Dh).transpose(0, 1, 3, 2, 4).reshape(B, s_pad, H, Dh)
    return out[:, :S]


def chunk_gated_delta_rule(q, k, v, g, beta):
    B, S, H, Dk = q.shape
    Dv = v.shape[-1]
    C = GDN_CHUNK
    n = S // C

    def blk(a):
        return jnp.moveaxis(a.reshape(B, n, C, H, *a.shape[3:]), 3, 1)

    q = blk(q * (Dk ** -0.5))
    k = blk(k)
    v = blk(v)
    g = blk(g)
    beta = blk(beta)
    g_cum = jnp.cumsum(g, axis=-1)
    idx = jnp.arange(C)
    incl = idx[:, None] >= idx[None, :]
    strict = idx[:, None] > idx[None, :]
    diff = g_cum[..., :, None] - g_cum[..., None, :]
    decay_incl = jnp.exp(jnp.where(incl, diff, NEG))
    decay_strict = jnp.where(strict, decay_incl, 0.0)

    k_beta = k * beta[..., None]
    lmat = jnp.einsum('bhncd,bhnjd->bhncj', k_beta, k) * decay_strict
    a_mat = lmat + jnp.eye(C, dtype=lmat.dtype)
    rhs = jnp.concatenate([v * beta[..., None], k_beta * jnp.exp(g_cum)[..., None]], axis=-1)
    sol = lax.linalg.triangular_solve(a_mat, rhs, left_side=True, lower=True, unit_diagonal=True)
    u = sol[..., :Dv]
    w = sol[..., Dv:]
    a_qk = jnp.einsum('bhncd,bhnjd->bhncj', q, k) * decay_incl

    xs = tuple(jnp.moveaxis(t, 2, 0) for t in (q, k, u, w, g_cum, a_qk))

    def step(state, inp):
        qi, ki, ui, wi, gi, aqk = inp
        v_new = ui - jnp.einsum('bhcd,bhde->bhce', wi, state)
        o = (jnp.einsum('bhcd,bhde->bhce', qi * jnp.exp(gi)[..., None], state)
             + jnp.einsum('bhcj,bhje->bhce', aqk, v_new))
        g_last = gi[..., -1]
        k_dec = ki * jnp.exp(g_last[..., None] - gi)[..., None]
        state = state * jnp.exp(g_last)[..., None, None] + jnp.einsum('bhcd,bhce->bhde', k_dec, v_new)
        return state, o

    s0 = jnp.zeros((B, H, Dk, Dv), jnp.float32)
    _, o = lax.scan(step, s0, xs)
    o = jnp.moveaxis(o, 0, 2)
    return jnp.moveaxis(o, 1, 3).reshape(B, S, H, Dv)


def moba_branch(h, w_in, w_out):
    B, S, _ = h.shape
    proj = h @ w_in
    q, k, v, z = jnp.split(proj, [MOBA_WIDTH, 2 * MOBA_WIDTH, 3 * MOBA_WIDTH], axis=-1)
    shp = (B, S, MOBA_HEADS, MOBA_HEAD_DIM)
    o = moba_attention(q.reshape(shp), k.reshape(shp), v.reshape(shp))
    o = o.reshape(B, S, MOBA_WIDTH) * jax.nn.silu(z)
    return o @ w_out


def gdn_branch(h, w_in, conv_w, a_log, dt_bias, norm_w, w_out):
    B, S, _ = h.shape
    W = GDN_WIDTH
    proj = h @ w_in
    qkv = proj[..., :3 * W].astype(jnp.float32)
    z = proj[..., 3 * W:4 * W].astype(jnp.float32)
    a = proj[..., 4 * W:4 * W + GDN_HEADS].astype(jnp.float32)
    bt = proj[..., 4 * W + GDN_HEADS:].astype(jnp.float32)
    qkv = jax.nn.silu(causal_depthwise_conv(qkv, conv_w.astype(jnp.float32)))
    shp = (B, S, GDN_HEADS, GDN_HEAD_DIM)
    q = l2_normalize(qkv[..., :W].reshape(shp))
    k = l2_normalize(qkv[..., W:2 * W].reshape(shp))
    v = qkv[..., 2 * W:].reshape(shp)
    g = -jnp.exp(a_log.astype(jnp.float32)) * jax.nn.softplus(a + dt_bias.astype(jnp.float32))
    beta = jax.nn.sigmoid(bt)
    o = chunk_gated_delta_rule(q, k, v, g, beta)
    o = o * lax.rsqrt(jnp.mean(jnp.square(o), axis=-1, keepdims=True) + RMS_EPS)
    o = o * norm_w.astype(jnp.float32) * jax.nn.silu(z.reshape(shp))
    return o.reshape(B, S, W).astype(h.dtype) @ w_out


def setup_inputs(seed: int = 0) -> dict:
    key = jax.random.key(seed)
    ks = jax.random.split(key, 16)
    f32 = jnp.float32
    x = jax.random.normal(ks[0], (BATCH, SEQ, D_MODEL), f32)
    c = jax.random.normal(ks[1], (BATCH, D_MODEL), f32)
    ada_w = jax.random.normal(ks[2], (DEPTH, D_MODEL, 3 * D_MODEL), f32) * (ADA_INIT_SCALE * D_MODEL ** -0.5)
    ada_base = jnp.concatenate([jnp.zeros((2 * D_MODEL,), f32), jnp.ones((D_MODEL,), f32)])
    ada_b = ada_base[None, :] + 0.01 * jax.random.normal(ks[3], (DEPTH, 3 * D_MODEL), f32)
    ln_g = 1.0 + 0.01 * jax.random.normal(ks[4], (DEPTH, D_MODEL), f32)
    ln_b = 0.01 * jax.random.normal(ks[5], (DEPTH, D_MODEL), f32)
    moba_w_in = jax.random.normal(ks[6], (N_MOBA, D_MODEL, MOBA_IN), f32) * D_MODEL ** -0.5
    moba_w_out = jax.random.normal(ks[7], (N_MOBA, MOBA_WIDTH, D_MODEL), f32) * (MOBA_WIDTH ** -0.5 * DEEPNORM_BETA)
    gdn_w_in = jax.random.normal(ks[8], (N_GDN, D_MODEL, GDN_IN), f32) * D_MODEL ** -0.5
    gdn_conv_w = jax.random.normal(ks[9], (N_GDN, GDN_CONV, 3 * GDN_WIDTH), f32) * GDN_CONV ** -0.5
    gdn_a_log = jnp.log(jax.random.uniform(ks[10], (N_GDN, GDN_HEADS), f32, 1.0, 16.0))
    dt = jnp.exp(jax.random.uniform(ks[11], (N_GDN, GDN_HEADS), f32, math.log(1e-3), math.log(1e-1)))
    gdn_dt_bias = dt + jnp.log(-jnp.expm1(-dt))
    gdn_norm_w = 1.0 + 0.01 * jax.random.normal(ks[12], (N_GDN, GDN_HEAD_DIM), f32)
    gdn_w_out = jax.random.normal(ks[13], (N_GDN, GDN_WIDTH, D_MODEL), f32) * (GDN_WIDTH ** -0.5 * DEEPNORM_BETA)
    return {"x": x, "c": c, "ada_w": ada_w, "ada_b": ada_b, "ln_g": ln_g, "ln_b": ln_b,
            "moba_w_in": moba_w_in, "moba_w_out": moba_w_out,
            "gdn_w_in": gdn_w_in, "gdn_conv_w": gdn_conv_w, "gdn_a_log": gdn_a_log,
            "gdn_dt_bias": gdn_dt_bias, "gdn_norm_w": gdn_norm_w, "gdn_w_out": gdn_w_out}


def reference(x, c, ada_w, ada_b, ln_g, ln_b, moba_w_in, moba_w_out,
              gdn_w_in, gdn_conv_w, gdn_a_log, gdn_dt_bias, gdn_norm_w, gdn_w_out):
    cs = jax.nn.silu(c)
    for i in range(DEPTH):
        mod = cs @ ada_w[i] + ada_b[i]
        shift, scale, gate = jnp.split(mod, 3, axis=-1)
        h = x * (1.0 + scale[:, None, :]) + shift[:, None, :]
        j = i // N_MIXERS
        if i % N_MIXERS == 0:
            y = moba_branch(h, moba_w_in[j], moba_w_out[j])
        else:
            y = gdn_branch(h, gdn_w_in[j], gdn_conv_w[j], gdn_a_log[j], gdn_dt_bias[j],
                           gdn_norm_w[j], gdn_w_out[j])
        x = layer_norm(DEEPNORM_ALPHA * x + gate[:, None, :] * y, ln_g[i], ln_b[i])
    return x
```

```python
import numpy as np
from contextlib import ExitStack
import concourse.bass as bass
import concourse.mybir as mybir
from concourse.bass_utils import run_bass_kernel_spmd

F32 = mybir.dt.float32
BF16 = mybir.dt.bfloat16
AF = mybir.ActivationFunctionType
ALU = mybir.AluOpType
AX = mybir.AxisListType

S_LEN = 2048
DM = 1024
NT = 16
ALPHA = 4.0 ** 0.25
LN_EPS = 1e-5
NEG = -1.0e30

EPOCH = 30000
NSEM_PER_ENG = 8


class Op:
    __slots__ = ("eng", "fn", "deps", "needed", "sem", "val", "dma_key", "gidx")


class Prog:
    ENGS = ("pe", "act", "dve", "pool", "sp")

    def __init__(self, nc, stack, same_sync=True):
        self.nc = nc
        self.stack = stack
        self.same_sync = same_sync
        self.streams = {e: [] for e in self.ENGS}
        self.last_w = {}
        self.readers = {}
        self.dma_sems = {}
        self.eng_sems = {}
        self.consts = set()
        for e in ("pe", "act", "dve", "pool"):
            self.eng_sems[e] = [stack.enter_context(nc.semaphore(f"s_{e}{i}")) for i in range(NSEM_PER_ENG)]
        self.out_dmas = []
        self.nbank = 0

    def sb(self, name, shape, dt, stack=None):
        self.nsb = getattr(self, "nsb", 0) + 1
        return (stack or self.stack).enter_context(self.nc.sbuf_tensor(f"sb{self.nsb}_{name}", list(shape), dt))

    def ps(self, name, shape, dt=F32):
        return self.stack.enter_context(self.nc.psum_tensor(name, list(shape), dt))

    def op(self, eng, fn, r=(), w=(), dma_key=None):
        o = Op()
        o.eng = eng
        o.fn = fn
        o.needed = False
        o.dma_key = dma_key
        o.deps = []
        o.sem = None
        o.val = 0
        o.gidx = 0
        deps = []
        for k in r:
            lw = self.last_w.get(k)
            if lw is not None:
                deps.append(lw)
        for k in w:
            lw = self.last_w.get(k)
            if lw is not None:
                deps.append(lw)
            deps.extend(self.readers.get(k, ()))
        pend = getattr(self, "pending", None)
        if pend and pend.get(eng):
            deps.extend(pend[eng])
            pend[eng] = []
        seen = set()
        for d in deps:
            if id(d) in seen:
                continue
            seen.add(id(d))
            if d.dma_key is None and d.eng == eng and (eng == "pe" or not self.same_sync):
                continue
            d.needed = True
            o.deps.append(d)
        for k in w:
            self.last_w[k] = o
            self.readers[k] = []
        for k in r:
            if k in w or k in self.consts:
                continue
            self.readers.setdefault(k, []).append(o)
        if dma_key is not None:
            ent = self.dma_sems.get(dma_key)
            if ent is None:
                ent = [self.stack.enter_context(self.nc.semaphore(f"d_{len(self.dma_sems)}")), 0]
                self.dma_sems[dma_key] = ent
            ent[1] += 1
            o.sem = ent[0]
            o.val = 16 * ent[1]
            o.needed = True
        self.streams[eng].append(o)
        return o

    def barrier(self):
        lasts = []
        for e in ("pe", "act", "dve", "pool"):
            for o in reversed(self.streams[e]):
                if o.dma_key is None and o.fn is not None:
                    lasts.append(o)
                    break
        self.pending = {e: list(lasts) for e in self.ENGS}

    def mark_const(self, *keys):
        for k in keys:
            self.consts.add(k)
            self.readers.pop(k, None)

    def dma(self, out, in_, r=(), w=(), key=None, eng="sp", is_out=False):
        o = self.op(eng, lambda e: e.dma_start(out=out, in_=in_), r=r, w=w, dma_key=key)
        if is_out:
            self.out_dmas.append(o)
        return o

    def mm(self, out, lhsT, rhs, start=True, stop=True, r=(), w=()):
        return self.op("pe", lambda e: e.matmul(out, lhsT, rhs, start=start, stop=stop), r=r, w=w)

    def tr(self, out, in_, ident, r=(), w=()):
        return self.op("pe", lambda e: e.transpose(out, in_, ident), r=r, w=w)

    def act(self, out, in_, func, bias=0.0, scale=1.0, accum=None, r=(), w=()):
        if accum is None:
            return self.op("act", lambda e: e.activation(out, in_, func, bias=bias, scale=scale), r=r, w=w)
        return self.op("act", lambda e: e.activation(out, in_, func, bias=bias, scale=scale, accum_out=accum), r=r, w=w)

    def ts(self, eng, out, in0, s1, s2, op0, op1=None, r=(), w=()):
        if op1 is None:
            return self.op(eng, lambda e: e.tensor_scalar(out, in0, s1, None, op0=op0), r=r, w=w)
        return self.op(eng, lambda e: e.tensor_scalar(out, in0, s1, s2, op0=op0, op1=op1), r=r, w=w)

    def tt(self, eng, out, in0, in1, op, r=(), w=()):
        return self.op(eng, lambda e: e.tensor_tensor(out, in0, in1, op=op), r=r, w=w)

    def stt(self, eng, out, in0, scalar, in1, op0, op1, r=(), w=()):
        return self.op(eng, lambda e: e.scalar_tensor_tensor(out, in0, scalar, in1, op0=op0, op1=op1), r=r, w=w)

    def copy(self, eng, out, in_, r=(), w=()):
        if eng == "act":
            return self.op("act", lambda e: e.copy(out, in_), r=r, w=w)
        return self.op(eng, lambda e: e.tensor_copy(out, in_), r=r, w=w)

    def memset(self, eng, ap, val, w=()):
        return self.op(eng, lambda e: e.memset(ap, val), w=w)

    def emit(self):
        nc = self.nc
        for e in self.ENGS:
            cnt = 0
            for o in self.streams[e]:
                if o.dma_key is None and o.needed:
                    assert e != "sp"
                    o.gidx = cnt
                    assert cnt // EPOCH < NSEM_PER_ENG, "too many signalling ops"
                    o.sem = self.eng_sems[e][cnt // EPOCH]
                    o.val = cnt % EPOCH + 1
                    cnt += 1
        fin = Op()
        fin.eng = "sp"
        fin.fn = None
        fin.deps = list(self.out_dmas)
        fin.needed = False
        fin.dma_key = None
        fin.sem = None
        fin.val = 0
        fin.gidx = 0
        self.streams["sp"].append(fin)
        stats = {}
        with nc.Block() as block:

            @block.tensor
            def _(eng):
                stats["pe"] = self._emit_stream("pe", eng)

            @block.scalar
            def _(eng):
                stats["act"] = self._emit_stream("act", eng)

            @block.vector
            def _(eng):
                stats["dve"] = self._emit_stream("dve", eng)

            @block.gpsimd
            def _(eng):
                stats["pool"] = self._emit_stream("pool", eng)

            @block.sync
            def _(eng):
                stats["sp"] = self._emit_stream("sp", eng)

        return stats

    def _emit_stream(self, e, eng):
        waited = {}
        nwait = 0
        nops = 0
        for o in self.streams[e]:
            for d in o.deps:
                if d.dma_key is not None:
                    k = ("d", d.dma_key)
                    v = d.val
                else:
                    k = ("e", d.eng)
                    v = d.gidx + 1
                if waited.get(k, 0) >= v:
                    continue
                eng.wait_ge(d.sem, d.val)
                waited[k] = v
                nwait += 1
            if o.fn is None:
                continue
            ins = o.fn(eng)
            nops += 1
            if o.dma_key is not None:
                ins.then_inc(o.sem, 16)
            elif o.needed:
                ins.then_inc(o.sem, 1)
        return (nops, nwait)


class Ctx:
    pass


def next_bank(C):
    b = C.bank_rr % 7
    C.bank_rr += 1
    return b


def next_stg(C):
    si = C.stg_rr % C.NSTG
    C.stg_rr += 1
    return si


def SK(si):
    return [f"stg{si}.{j}" for j in range(8)]


def setup_common(P, D):
    C = Ctx()
    C.P = P
    C.D = D
    C.pb = [P.ps(f"pb{i}", [128, 512], F32) for i in range(7)]
    C.pbT = P.ps("pbT", [128, 1024], BF16)
    C.bank_rr = 0
    C.NSTG = 5
    C.stg = [P.sb(f"stg{i}", [128, 1024], F32) for i in range(C.NSTG)]
    C.stg_rr = 0

    cst = P.sb("cst", [128, 512], F32)
    P.dma(cst[:], D["cst"][:, :], w=["cst"], key="cst")
    P.mark_const("cst")
    C.cst = cst
    C.ident = cst[:, 0:128]
    C.triu = cst[:, 128:256]
    C.trilS = cst[:, 256:384]
    C.ones = cst[:, 384:512]
    C.identb = P.sb("identb", [128, 128], BF16)
    C.triub = P.sb("triub", [128, 128], BF16)
    P.copy("pool", C.identb[:], C.ident, r=["cst"], w=["identb"])
    P.copy("pool", C.triub[:], C.triu, r=["cst"], w=["triub"])
    P.mark_const("identb", "triub")
    oh4 = P.sb("oh4", [4, 512], F32)
    P.dma(oh4[:], D["oh4"][:, :], w=["oh4"], key="oh4")
    P.mark_const("oh4")
    C.oh4 = oh4

    modg = P.sb("modg", [4, 2048], F32)
    C.ssT = P.sb("ssT", [128, 128], F32)
    C.lnB = P.sb("lnB", [128, 4096], F32)
    bk = C.pb[6]
    with ExitStack() as ts_:
        c_sb = P.sb("c_sb", [4, 1024], F32, ts_)
        cs_sb = P.sb("cs_sb", [4, 1024], F32, ts_)
        csT = P.sb("csT", [128, 32], F32, ts_)
        adab = P.sb("adab", [12, 512], F32, ts_)
        oh12 = P.sb("oh12", [12, 48], F32, ts_)
        modss = P.sb("modss", [4, 2048], F32, ts_)
        lnrow = P.sb("lnrow", [4, 1024], F32, ts_)
        P.dma(c_sb[:], D["c4"][:, :], w=["c_sb"], key="c_sb")
        P.dma(adab[:], D["ada_b"][:, :], w=["adab"], key="adab")
        P.dma(oh12[:], D["oh12"][:, :], w=["oh12"], key="oh12")
        P.dma(lnrow[0:2, :], D["ln_g"][:, :], w=["lnrow0"], key="lnrow0")
        P.dma(lnrow[2:4, :], D["ln_b"][:, :], w=["lnrow1"], key="lnrow1")
        P.act(cs_sb[:], c_sb[:], AF.Silu, r=["c_sb"], w=["cs_sb"])
        for c in range(8):
            P.tr(bk[:, c * 4:(c + 1) * 4], cs_sb[0:4, c * 128:(c + 1) * 128], cst[0:4, 0:4], r=["cs_sb", "cst"], w=["pb6"])
        P.copy("dve", csT[:], bk[:, 0:32], r=[], w=["csT", "pb6"])
        for l in range(2):
            for c in range(8):
                for th in range(3):
                    si = next_stg(C)
                    P.dma(C.stg[si][:], D["ada_w"][l * 1024 + c * 128: l * 1024 + (c + 1) * 128, th * 1024:(th + 1) * 1024],
                          w=SK(si), key=f"stg{si}")
                    for hf in range(2):
                        b = th * 2 + hf
                        P.mm(C.pb[b][0:4, :], csT[:, c * 4:(c + 1) * 4], C.stg[si][:, hf * 512:(hf + 1) * 512],
                             start=(c == 0), stop=False, r=SK(si) + ["csT"], w=[f"pb{b}"])
            for b in range(6):
                k = l * 6 + b
                P.mm(C.pb[b][0:4, :], oh12[:, k * 4:(k + 1) * 4], adab[:, :],
                     start=False, stop=True, r=["adab", "oh12"], w=[f"pb{b}"])
            for b in range(4):
                P.copy("dve" if b % 2 else "act", modss[:, b * 512:(b + 1) * 512], C.pb[b][0:4, :], r=[], w=["modss", f"pb{b}"])
            for b in range(4, 6):
                P.copy("dve" if b % 2 else "act", modg[:, l * 1024 + (b - 4) * 512: l * 1024 + (b - 3) * 512], C.pb[b][0:4, :],
                       r=[], w=["modg", f"pb{b}"])
            for k in range(16):
                P.tr(bk[:, k * 4:(k + 1) * 4], modss[0:4, k * 128:(k + 1) * 128], cst[0:4, 0:4], r=["modss", "cst"], w=["pb6"])
            P.copy("dve", C.ssT[:, l * 64: l * 64 + 32], bk[:, 0:32], r=[], w=["ssT", "pb6"])
            P.ts("dve", C.ssT[:, l * 64 + 32: l * 64 + 64], bk[:, 32:64], 1.0, None, ALU.add, r=[], w=["ssT", "pb6"])
        for k in range(8):
            b = k % 6
            rr = k // 2
            hf = k % 2
            P.mm(C.pb[b][:, :], oh4[0:4, rr * 128:(rr + 1) * 128], lnrow[0:4, hf * 512:(hf + 1) * 512],
                 r=["lnrow0", "lnrow1", "oh4"], w=[f"pb{b}"])
            P.copy("dve" if k % 2 else "act", C.lnB[:, k * 512:(k + 1) * 512], C.pb[b][:, :], r=[], w=["lnB", f"pb{b}"])
        C.prologue_keys = ["c_sb", "cs_sb", "csT", "adab", "oh12", "modss", "lnrow0", "lnrow1"]
    C.modg = modg
    C.gateB = P.sb("gateB", [128, 1024], F32)
    return C


def make_gateB(C, l, b):
    P = C.P
    for hf in range(2):
        bk = next_bank(C)
        P.mm(C.pb[bk][:, :], C.oh4[0:4, b * 128:(b + 1) * 128], C.modg[0:4, l * 1024 + hf * 512: l * 1024 + (hf + 1) * 512],
             r=["oh4", "modg"], w=[f"pb{bk}"])
        P.copy("act", C.gateB[:, hf * 512:(hf + 1) * 512], C.pb[bk][:, :], r=[], w=["gateB", f"pb{bk}"])


def build_hT(C, l, b, xin, row0, hT, tin="x"):
    P = C.P
    for tg in range(4):
        sis = []
        for j in range(4):
            t = tg * 4 + j
            si = next_stg(C)
            sis.append(si)
            P.dma(C.stg[si][:], xin[row0 + t * 128: row0 + (t + 1) * 128, :], r=[f"X{tin}.{row0 + t * 128}"], w=SK(si), key=f"stg{si}")
        for c in range(8):
            bk = next_bank(C)
            for j in range(4):
                P.tr(C.pb[bk][:, j * 128:(j + 1) * 128], C.stg[sis[j]][:, c * 128:(c + 1) * 128], C.ident,
                     r=SK(sis[j]) + ["cst"], w=[f"pb{bk}"])
            sh = C.ssT[:, l * 64 + c * 4 + b: l * 64 + c * 4 + b + 1]
            sc = C.ssT[:, l * 64 + 32 + c * 4 + b: l * 64 + 32 + c * 4 + b + 1]
            dst = hT[:, c * 2048 + tg * 512: c * 2048 + (tg + 1) * 512]
            if c % 2 == 0:
                P.ts("dve", dst, C.pb[bk][:, :], sc, sh, ALU.mult, ALU.add, r=["ssT"], w=[f"hT{c}.{tg}", f"pb{bk}"])
            else:
                P.act(dst, C.pb[bk][:, :], AF.Identity, bias=sh, scale=sc, r=["ssT"], w=[f"hT{c}.{tg}", f"pb{bk}"])


def load_w_bf16(C, wsrc, rows0, col0, ncols, dst, dst_off, dkey, nchunk=8, cast_eng="pool"):
    P = C.P
    per = 1024 // ncols
    c = 0
    while c < nchunk:
        n = min(per, nchunk - c)
        si = next_stg(C)
        sk = f"stg{si}"
        for j in range(n):
            P.dma(C.stg[si][:, j * ncols:(j + 1) * ncols],
                  wsrc[rows0 + (c + j) * 128: rows0 + (c + j + 1) * 128, col0:col0 + ncols],
                  w=(SK(si) if n == 1 else [f"stg{si}.{j}"]), key=sk)
        P.copy(cast_eng, dst[:, dst_off + c * ncols: dst_off + (c + n) * ncols], C.stg[si][:, 0:n * ncols], r=SK(si), w=[dkey])
        c += n


def out_proj_ln(C, l, b, oT, wbig, xres, xout, row0, tin="x", tout="out"):
    P = C.P
    for tt in range(NT):
        bks = [next_bank(C), next_bank(C)]
        for hf in range(2):
            for h in range(8):
                P.mm(C.pb[bks[hf]][:, :], oT[:, h * 2048 + tt * 128: h * 2048 + (tt + 1) * 128],
                     wbig[:, h * 1024 + hf * 512: h * 1024 + (hf + 1) * 512], start=(h == 0), stop=(h == 7),
                     r=[f"oT{h}", "wbig"], w=[f"pb{bks[hf]}"])
        sx = next_stg(C)
        s1 = next_stg(C)
        P.dma(C.stg[sx][:], xres[row0 + tt * 128: row0 + (tt + 1) * 128, :], r=[f"X{tin}.{row0 + tt * 128}"], w=SK(sx), key=f"stg{sx}")
        t1 = C.stg[s1]
        for hf in range(2):
            P.tt("dve", t1[:, hf * 512:(hf + 1) * 512], C.pb[bks[hf]][:, :], C.gateB[:, hf * 512:(hf + 1) * 512], ALU.mult,
                 r=["gateB"], w=SK(s1) + [f"pb{bks[hf]}"])
        P.ts("pool", C.stg[sx][:], C.stg[sx][:], ALPHA, None, ALU.mult, r=[], w=SK(sx))
        P.tt("pool", t1[:], t1[:], C.stg[sx][:], ALU.add, r=SK(sx), w=SK(s1))
        st = C.lnst
        P.op("dve", lambda e, t1=t1, st=st: e.bn_stats(st[:, 0:6], t1[:, 0:512]), r=SK(s1), w=["lnst"])
        P.op("dve", lambda e, t1=t1, st=st: e.bn_stats(st[:, 6:12], t1[:, 512:1024]), r=SK(s1), w=["lnst"])
        P.op("dve", lambda e, st=st: e.bn_aggr(st[:, 12:14], st[:, 0:12].rearrange("p (a b) -> p a b", a=2)), r=[], w=["lnst"])
        P.act(st[:, 14:15], st[:, 13:14], AF.Sqrt, bias=LN_EPS, scale=1.0, r=[], w=["lnst"])
        P.op("dve", lambda e, st=st: e.reciprocal(st[:, 15:16], st[:, 14:15]), r=[], w=["lnst"])
        P.stt("dve", st[:, 16:17], st[:, 12:13], -1.0, st[:, 15:16], ALU.mult, ALU.mult, r=[], w=["lnst"])
        xn = C.stg[sx]
        P.act(xn[:], t1[:], AF.Identity, bias=st[:, 16:17], scale=st[:, 15:16], r=SK(s1) + ["lnst"], w=SK(sx))
        P.tt("pool", xn[:], xn[:], C.lnB[:, l * 1024:(l + 1) * 1024], ALU.mult, r=["lnB"], w=SK(sx))
        P.tt("pool", xn[:], xn[:], C.lnB[:, 2048 + l * 1024: 2048 + (l + 1) * 1024], ALU.add, r=["lnB"], w=SK(sx))
        P.dma(xout[row0 + tt * 128: row0 + (tt + 1) * 128, :], xn[:], r=SK(sx), w=[f"X{tout}.{row0 + tt * 128}"], key=f"stg{sx}", is_out=True)


def layer0(C, nseq, xin, xout, tin="x", tout="out"):
    P = C.P
    D = C.D
    with ExitStack() as ls:
        hT = P.sb("hT", [128, 8 * 2048], BF16, ls)
        vall = P.sb("vall", [128, NT * 8 * 129], BF16, ls)
        oT = P.sb("oT", [128, 8 * 2048], BF16, ls)
        wbig = P.sb("wbig", [128, 8 * 1024], BF16, ls)
        wh = [P.sb(f"wh{i}", [128, 8 * 384], BF16, ls) for i in range(2)]
        qT = P.sb("qT", [128, 2048], BF16, ls)
        kT = P.sb("kT", [128, 2048], BF16, ls)
        zT = P.sb("zT", [128, 2048], BF16, ls)
        kms = P.sb("kms", [128, 8], F32, ls)
        kmb = P.sb("kmb", [128, 8], BF16, ls)
        gm = P.sb("gm", [128, 128], F32, ls)
        top8 = P.sb("top8", [128, 8], F32, ls)
        msk = P.sb("msk", [128, 128], F32, ls)
        NPT = 4
        PT = [P.sb(f"PT{i}", [128, 512], BF16, ls) for i in range(NPT)]
        acc = P.sb("acc", [128, 2 * 129], F32, ls)
        rden = P.sb("rden", [128, 2], F32, ls)
        onb = [P.sb(f"onb{i}", [128, 128], BF16, ls) for i in range(2)]
        C.lnst = P.sb("lnst0", [128, 32], F32, ls)
        vv = vall[:, :].rearrange("p (t h e) -> p t h e", t=NT, h=8)
        P.memset("pool", vall[:, :], 1.0, w=[f"v{tt}" for tt in range(NT)])
        P.memset("pool", gm[:, :], NEG, w=["gm"])
        P.memset("pool", msk[:, :], 1.0, w=["msk"])
        scale = 128.0 ** -0.5
        pt_rr = 0
        for s in range(nseq):
            b = s
            row0 = s * S_LEN
            make_gateB(C, 0, b)
            build_hT(C, 0, b, xin, row0, hT, tin)
            hkeys = lambda tq: [f"hT{c}.{tq}" for c in range(8)]
            load_w_bf16(C, D["moba_w_in"], 0, 2048, 1024, wbig, 0, "wbig")
            for tt in range(NT):
                for hf in range(2):
                    bk = next_bank(C)
                    for c in range(8):
                        P.mm(C.pb[bk][:, :], hT[:, c * 2048 + tt * 128: c * 2048 + (tt + 1) * 128],
                             wbig[:, c * 1024 + hf * 512: c * 1024 + (hf + 1) * 512], start=(c == 0), stop=(c == 7),
                             r=[f"hT{c}.{tt // 4}", "wbig"], w=[f"pb{bk}"])
                    dst = vv[:, tt, hf * 4:(hf + 1) * 4, 0:128]
                    src = C.pb[bk][:, :].rearrange("p (h e) -> p h e", h=4)
                    P.copy("act" if hf else "dve", dst, src, r=[], w=[f"v{tt}", f"pb{bk}"])
            for h in range(8):
                whh = wh[h % 2]
                wk_ = f"wh{h % 2}"
                for mi, col0 in enumerate((h * 128, 1024 + h * 128, 3072 + h * 128)):
                    load_w_bf16(C, D["moba_w_in"], 0, col0, 128, whh, mi * 1024, wk_)
                for mi, (dstT, dk) in enumerate(((qT, "qT"), (kT, "kT"), (zT, "zT"))):
                    for tq in range(4):
                        bk = next_bank(C)
                        for c in range(8):
                            P.mm(C.pb[bk][:, :], whh[:, mi * 1024 + c * 128: mi * 1024 + (c + 1) * 128],
                                 hT[:, c * 2048 + tq * 512: c * 2048 + (tq + 1) * 512], start=(c == 0), stop=(c == 7),
                                 r=[wk_, f"hT{c}.{tq}"], w=[f"pb{bk}"])
                        dst = dstT[:, tq * 512:(tq + 1) * 512]
                        if mi == 0:
                            P.copy("act", dst, C.pb[bk][:, :], r=[], w=[f"qT{tq}", f"pb{bk}"])
                        elif mi == 1:
                            P.copy("dve", dst, C.pb[bk][:, :], r=[], w=[f"kT{tq}", f"pb{bk}"])
                            P.op("dve", lambda e, o_=kms[:, 2 * tq:2 * tq + 2], i_=C.pb[bk][:, :].rearrange("p (a b) -> p a b", a=2):
                                 e.tensor_reduce(o_, i_, axis=AX.X, op=ALU.add), r=[], w=["kms", f"pb{bk}"])
                        else:
                            P.act(dst, C.pb[bk][:, :], AF.Silu, r=[], w=[f"zT{tq}", f"pb{bk}"])
                P.copy("dve", kmb[:, :], kms[:, :], r=["kms"], w=["kmb"])
                bg = next_bank(C)
                for qt in range(8, NT):
                    P.mm(C.pb[bg][:, qt * 8:(qt + 1) * 8], qT[:, qt * 128:(qt + 1) * 128], kmb[:, :],
                         r=[f"qT{qt // 4}", "kmb"], w=[f"pb{bg}"])
                for i in range(4, 8):
                    src = C.pb[bg][:, i * 16:(i + 1) * 16].rearrange("p (a b) -> p a b", a=2)[:, :, 0:i]
                    dst = gm[:, i * 16:(i + 1) * 16].rearrange("p (a b) -> p a b", a=2)[:, :, 0:i]
                    P.copy("dve", dst, src, r=[], w=["gm", f"pb{bg}"])
                for qt in range(8, NT):
                    i = qt // 2
                    P.op("dve", lambda e, o_=top8[:, :], i_=gm[:, qt * 8:(qt + 1) * 8]: e.max(out=o_, in_=i_), r=["gm"], w=["top8"])
                    P.ts("dve", msk[:, qt * 8: qt * 8 + i], gm[:, qt * 8: qt * 8 + i], top8[:, 2:3], None, ALU.is_ge,
                         r=["gm", "top8"], w=["msk"])
                for i in range(8):
                    qa, qb = 2 * i, 2 * i + 1
                    qsl = slice(i * 256, (i + 1) * 256)
                    qk = f"qT{i // 2}"

                    def scores(n):
                        nonlocal pt_rr
                        bk = next_bank(C)
                        for j in range(2):
                            kt = 2 * n + j
                            P.mm(C.pb[bk][:, j * 256:(j + 1) * 256], kT[:, kt * 128:(kt + 1) * 128], qT[:, qsl],
                                 r=[f"kT{kt // 4}", qk], w=[f"pb{bk}"])
                        pi = pt_rr % NPT
                        pt_rr += 1
                        P.act(PT[pi][:, :], C.pb[bk][:, :], AF.Exp, scale=scale, r=[], w=[f"PT{pi}", f"pb{bk}"])
                        if n == i:
                            P.tt("pool", PT[pi][:, 0:128], PT[pi][:, 0:128], C.triub[:, :], ALU.mult, r=["triub"], w=[f"PT{pi}"])
                            P.tt("pool", PT[pi][:, 384:512], PT[pi][:, 384:512], C.triub[:, :], ALU.mult, r=["triub"], w=[f"PT{pi}"])
                        return pi

                    def pv(bko, pi, n, first, last):
                        for qi, qt in enumerate((qa, qb)):
                            kts = [0, 1]
                            if n == i and qi == 0:
                                kts = [0]
                            for j in kts:
                                kt = 2 * n + j
                                P.mm(C.pb[bko][:, qi * 256: qi * 256 + 129], PT[pi][:, j * 256 + qi * 128: j * 256 + (qi + 1) * 128],
                                     vv[:, kt, h, :], start=(first and j == 0), stop=(last and j == kts[-1]),
                                     r=[f"PT{pi}", f"v{kt}"], w=[f"pb{bko}"])

                    def finish(qi, qt, src_ap, src_keys_r, src_keys_w):
                        P.op("dve", lambda e, o_=rden[:, qi:qi + 1], i_=src_ap[:, 128:129]: e.reciprocal(o_, i_),
                             r=src_keys_r, w=["rden"] + src_keys_w)
                        ob = onb[qi]
                        P.ts("dve", ob[:, :], src_ap[:, 0:128], rden[:, qi:qi + 1], None, ALU.mult,
                             r=["rden"] + src_keys_r, w=[f"onb{qi}"] + src_keys_w)
                        P.tr(C.pbT[:, qi * 128:(qi + 1) * 128], ob[:, :], C.identb[:, :], r=[f"onb{qi}", "identb"], w=["pbT"])
                        P.tt("dve", oT[:, h * 2048 + qt * 128: h * 2048 + (qt + 1) * 128], C.pbT[:, qi * 128:(qi + 1) * 128],
                             zT[:, qt * 128:(qt + 1) * 128], ALU.mult, r=[f"zT{qt // 4}"], w=[f"oT{h}", "pbT"])

                    if i <= 3:
                        blocks = list(range(i, -1, -1))
                        pis = {}
                        bko = next_bank(C)
                        for n in blocks:
                            pis[n] = scores(n)
                        for qi, qt in enumerate((qa, qb)):
                            mms = []
                            for n in blocks:
                                kts = [0, 1]
                                if n == i and qi == 0:
                                    kts = [0]
                                for j in kts:
                                    mms.append((n, j))
                            for idx, (n, j) in enumerate(mms):
                                kt = 2 * n + j
                                pi = pis[n]
                                P.mm(C.pb[bko][:, qi * 256: qi * 256 + 129], PT[pi][:, j * 256 + qi * 128: j * 256 + (qi + 1) * 128],
                                     vv[:, kt, h, :], start=(idx == 0), stop=(idx == len(mms) - 1),
                                     r=[f"PT{pi}", f"v{kt}"], w=[f"pb{bko}"])
                        for qi, qt in enumerate((qa, qb)):
                            finish(qi, qt, C.pb[bko][:, qi * 256: qi * 256 + 129], [], [f"pb{bko}"])
                    else:
                        order = [i] + list(range(i))
                        pend = None
                        for idx, n in enumerate(order):
                            pi = scores(n)
                            bko = next_bank(C)
                            pv(bko, pi, n, True, True)
                            for qi, qt in enumerate((qa, qb)):
                                src = C.pb[bko][:, qi * 256: qi * 256 + 129]
                                dst = acc[:, qi * 129:(qi + 1) * 129]
                                if idx == 0:
                                    P.copy("dve", dst, src, r=[], w=[f"acc{qi}", f"pb{bko}"])
                                else:
                                    P.stt("dve", dst, src, msk[:, qt * 8 + n: qt * 8 + n + 1], dst, ALU.mult, ALU.add,
                                          r=["msk"], w=[f"acc{qi}", f"pb{bko}"])
                        for qi, qt in enumerate((qa, qb)):
                            finish(qi, qt, acc[:, qi * 129:(qi + 1) * 129], [f"acc{qi}"], [])
            load_w_bf16(C, D["moba_w_out"], 0, 0, 1024, wbig, 0, "wbig")
            out_proj_ln(C, 0, b, oT, wbig, xin, xout, row0, tin, tout)


def layer1(C, nseq, xin, xout, tin="x", tout="out"):
    P = C.P
    D = C.D
    W = D["gdn_w_in"]
    with ExitStack() as ls:
        hT = P.sb("hT1", [128, 8 * 2048], BF16, ls)
        oT = P.sb("oT1", [128, 8 * 2048], BF16, ls)
        C.lnst = P.sb("lnst1", [128, 32], F32, ls)
        convT = P.sb("convT", [128, 96], F32, ls)
        nwB = P.sb("nwB", [128, 128], F32, ls)
        ealogB = P.sb("ealogB", [128, 128], F32, ls)
        dtbB = P.sb("dtbB", [128, 128], F32, ls)
        gp = P.sb("gp", [128, 128], F32, ls)
        beta = P.sb("beta", [128, 128], F32, ls)
        Gcol = P.sb("Gcol", [128, 128], F32, ls)
        egc = P.sb("egc", [128, 128], F32, ls)
        egd = P.sb("egd", [128, 128], F32, ls)
        egl = P.sb("egl", [128, 128], F32, ls)
        with ExitStack() as ts_:
            cw = P.sb("cw", [4, 3072], F32, ts_)
            ab2 = P.sb("ab2", [2, 8], F32, ts_)
            nwr = P.sb("nwr", [1, 128], F32, ts_)
            P.dma(cw[:], D["gdn_conv_w"][:, :], w=["cw"], key="cw")
            P.dma(ab2[0:1, :], D["gdn_a_log"][:, :], w=["ab2a"], key="ab2a")
            P.dma(ab2[1:2, :], D["gdn_dt_bias"][:, :], w=["ab2b"], key="ab2b")
            P.dma(nwr[:], D["gdn_norm_w"][:, :], w=["nwr"], key="nwr")
            bk = next_bank(C)
            for g in range(24):
                P.tr(C.pb[bk][:, g * 4:(g + 1) * 4], cw[0:4, g * 128:(g + 1) * 128], C.cst[0:4, 0:4], r=["cw", "cst"], w=[f"pb{bk}"])
            P.copy("dve", convT[:, :], C.pb[bk][:, 0:96], r=[], w=["convT", f"pb{bk}"])
            bk = next_bank(C)
            P.mm(C.pb[bk][:, 0:8], C.oh4[0:2, 0:128], ab2[0:2, :], r=["ab2a", "ab2b", "oh4"], w=[f"pb{bk}"])
            P.mm(C.pb[bk][:, 8:16], C.oh4[0:2, 128:256], ab2[0:2, :], r=["ab2a", "ab2b", "oh4"], w=[f"pb{bk}"])
            P.mm(C.pb[bk][:, 128:256], C.oh4[0:1, 0:128], nwr[0:1, :], r=["nwr", "oh4"], w=[f"pb{bk}"])
            for t in range(NT):
                P.act(ealogB[:, t * 8:(t + 1) * 8], C.pb[bk][:, 0:8], AF.Exp, r=[], w=["ealogB", f"pb{bk}"])
                P.copy("dve", dtbB[:, t * 8:(t + 1) * 8], C.pb[bk][:, 8:16], r=[], w=["dtbB", f"pb{bk}"])
            P.copy("dve", nwB[:, :], C.pb[bk][:, 128:256], r=[], w=["nwB", f"pb{bk}"])
        P.mark_const("convT", "nwB", "ealogB", "dtbB")
        DKS = 128.0 ** -0.5
        for s in range(nseq):
            b = s
            row0 = s * S_LEN
            make_gateB(C, 1, b)
            build_hT(C, 1, b, xin, row0, hT, tin)
            P.barrier()
            with ExitStack() as hs:
                wab = P.sb("wab", [128, 128], BF16, hs)
                absb = P.sb("absb", [128, 256], F32, hs)
                tmpg = P.sb("tmpg", [128, 128], F32, hs)
                wh = [P.sb(f"w1h{i}", [128, 4 * 1024], BF16, hs) for i in range(1)]
                uT = P.sb("uT", [128, 2052], BF16, hs)
                sT = P.sb("sT", [128, 2048], BF16, hs)
                k_tm = P.sb("k_tm", [128, 2048], BF16, hs)
                q_tm = P.sb("q_tm", [128, 2048], BF16, hs)
                vb = P.sb("vb", [128, 2048], BF16, hs)
                kbg = P.sb("kbg", [128, 2048], BF16, hs)
                kdec = P.sb("kdec", [128, 2048], BF16, hs)
                tmq = P.sb("tmq", [128, 512], BF16, hs)
                knT = P.sb("knT", [128, 2048], BF16, hs)
                qnT = P.sb("qnT", [128, 2048], BF16, hs)
                qgT = P.sb("qgT", [128, 2048], BF16, hs)
                zs = P.sb("zs", [128, 2048], BF16, hs)
                dgw = P.sb("dgw", [128, 12 * 128], BF16, hs)
                u_sb = P.sb("u_sb", [128, 2048], F32, hs)
                wT_sb = P.sb("wT_sb", [128, 2048], BF16, hs)
                Aq_sb = P.sb("Aq_sb", [128, 2048], BF16, hs)
                sc = P.sb("sc1", [128, 16 * 12], F32, hs)
                junk = P.sb("junk", [128, 128], BF16, hs)
                gtri = [P.sb(f"gtri{i}", [128, 128], F32, hs) for i in range(2)]
                dab = [P.sb(f"dab{i}", [128, 128], F32, hs) for i in range(2)]
                EL = [P.sb(f"EL{i}", [128, 128], F32, hs) for i in range(2)]
                EU = [P.sb(f"EU{i}", [128, 128], F32, hs) for i in range(2)]
                QX = [P.sb(f"QX{i}", [128, 256], F32, hs) for i in range(2)]
                RR = [P.sb(f"RR{i}", [128, 128], F32, hs) for i in range(2)]
                XTb = [P.sb(f"XTb{i}", [128, 128], BF16, hs) for i in range(2)]
                Sst = P.sb("Sst", [128, 128], F32, hs)
                Sbf = P.sb("Sbf", [128, 128], BF16, hs)
                vnew = [P.sb(f"vnew{i}", [128, 128], BF16, hs) for i in range(2)]
                tno = [P.sb(f"tno{i}", [128, 128], F32, hs) for i in range(2)]
                onb = [P.sb(f"on1b{i}", [128, 128], BF16, hs) for i in range(2)]
                P.memset("pool", uT[:, 0:4], 0.0, w=["uT"])
                load_w_bf16(C, W, 0, 4096, 16, wab, 0, "wab")
                bk = next_bank(C)
                for tt in range(NT):
                    for c in range(8):
                        P.mm(C.pb[bk][:, tt * 16:(tt + 1) * 16], hT[:, c * 2048 + tt * 128: c * 2048 + (tt + 1) * 128],
                             wab[:, c * 16:(c + 1) * 16], start=(c == 0), stop=(c == 7), r=[f"hT{c}.{tt // 4}", "wab"], w=[f"pb{bk}"])
                P.copy("dve", absb[:, :], C.pb[bk][:, 0:256], r=[], w=["absb", f"pb{bk}"])
                av = absb[:, :].rearrange("p (t e) -> p t e", t=NT)
                P.tt("dve", tmpg[:, :].rearrange("p (t e) -> p t e", t=NT), av[:, :, 0:8],
                     dtbB[:, :].rearrange("p (t e) -> p t e", t=NT), ALU.add, r=["absb", "dtbB"], w=["tmpg"])
                P.act(tmpg[:, :], tmpg[:, :], AF.Exp, r=[], w=["tmpg"])
                P.act(tmpg[:, :], tmpg[:, :], AF.Ln, bias=1.0, r=[], w=["tmpg"])
                P.tt("dve", gp[:, :], tmpg[:, :], ealogB[:, :], ALU.mult, r=["tmpg", "ealogB"], w=["gp"])
                P.act(beta[:, :].rearrange("p (t e) -> p t e", t=NT), av[:, :, 8:16], AF.Sigmoid, r=["absb"], w=["beta"])
                bk = next_bank(C)
                P.mm(C.pb[bk][:, 0:128], C.triu, gp[:, :], r=["cst", "gp"], w=[f"pb{bk}"])
                P.mm(C.pb[bk][:, 128:256], C.ones, gp[:, :], r=["cst", "gp"], w=[f"pb{bk}"])
                P.copy("dve", Gcol[:, :], C.pb[bk][:, 0:128], r=[], w=["Gcol", f"pb{bk}"])
                P.act(egc[:, :], C.pb[bk][:, 0:128], AF.Exp, scale=-1.0, r=[], w=["egc", f"pb{bk}"])
                P.act(egl[:, :], C.pb[bk][:, 128:256], AF.Exp, scale=-1.0, r=[], w=["egl", f"pb{bk}"])
                P.tt("dve", tmpg[:, :], Gcol[:, :], C.pb[bk][:, 128:256], ALU.subtract, r=["Gcol"], w=["tmpg", f"pb{bk}"])
                P.act(egd[:, :], tmpg[:, :], AF.Exp, r=["tmpg"], w=["egd"])
                betav = beta[:, :].rearrange("p (t e) -> p t e", t=NT)
                egcv = egc[:, :].rearrange("p (t e) -> p t e", t=NT)
                egdv = egd[:, :].rearrange("p (t e) -> p t e", t=NT)
                for h in range(8):
                    whh = wh[0]
                    wk_ = "w1h0"
                    for mi in range(4):
                        load_w_bf16(C, W, 0, mi * 1024 + h * 128, 128, whh, mi * 1024, wk_)
                    P.memset("pool", sc[:, :], 0.0, w=["ssq", "ssk", "sso", "scA", "scB", "scC"])
                    for wi in range(3):
                        for j in range(4):
                            col = (wi * 8 + h) * 4 + j
                            P.ts("dve", dgw[:, (wi * 4 + j) * 128:(wi * 4 + j + 1) * 128], C.identb[:, :], convT[:, col:col + 1], None,
                                 ALU.mult, r=["identb", "convT"], w=["dgw"])
                    for tq in range(4):
                        bk = next_bank(C)
                        for j in range(4):
                            tt = tq * 4 + j
                            for c in range(8):
                                P.mm(C.pb[bk][:, j * 128:(j + 1) * 128], hT[:, c * 2048 + tt * 128: c * 2048 + (tt + 1) * 128],
                                     whh[:, 3 * 1024 + c * 128: 3 * 1024 + (c + 1) * 128], start=(c == 0), stop=(c == 7),
                                     r=[f"hT{c}.{tq}", wk_], w=[f"pb{bk}"])
                        P.act(zs[:, tq * 512:(tq + 1) * 512], C.pb[bk][:, :], AF.Silu, r=[], w=["zs", f"pb{bk}"])
                    for wi in range(3):
                        for tq in range(4):
                            bk = next_bank(C)
                            for c in range(8):
                                P.mm(C.pb[bk][:, :], whh[:, wi * 1024 + c * 128: wi * 1024 + (c + 1) * 128],
                                     hT[:, c * 2048 + tq * 512: c * 2048 + (tq + 1) * 512], start=(c == 0), stop=(c == 7),
                                     r=[wk_, f"hT{c}.{tq}"], w=[f"pb{bk}"])
                            P.copy("act" if tq % 2 else "dve", uT[:, 3 + tq * 512: 3 + (tq + 1) * 512], C.pb[bk][:, :], r=[], w=["uT", f"pb{bk}"])
                        for tq in range(4):
                            bk = next_bank(C)
                            for j in range(4):
                                P.mm(C.pb[bk][:, :], dgw[:, (wi * 4 + j) * 128:(wi * 4 + j + 1) * 128],
                                     uT[:, tq * 512 + j: tq * 512 + j + 512], start=(j == 0), stop=(j == 3), r=["dgw", "uT"], w=[f"pb{bk}"])
                            P.act(sT[:, tq * 512:(tq + 1) * 512], C.pb[bk][:, :], AF.Silu, r=[], w=["sT", f"pb{bk}"])
                        for tp in range(2):
                            for j in range(8):
                                tt = tp * 8 + j
                                P.tr(C.pbT[:, j * 128:(j + 1) * 128], sT[:, tt * 128:(tt + 1) * 128], C.identb[:, :], r=["sT", "identb"], w=["pbT"])
                            for j in range(8):
                                tt = tp * 8 + j
                                src = C.pbT[:, j * 128:(j + 1) * 128]
                                if wi == 0:
                                    P.copy("dve", q_tm[:, tt * 128:(tt + 1) * 128], src, r=[], w=["q_tm", "pbT"])
                                    P.act(junk[:, :], src, AF.Square, accum=sc[:, 0 * 16 + tt: 0 * 16 + tt + 1], r=[], w=["junk", "ssq", "pbT"])
                                elif wi == 1:
                                    P.copy("dve", k_tm[:, tt * 128:(tt + 1) * 128], src, r=[], w=["k_tm", "pbT"])
                                    P.act(junk[:, :], src, AF.Square, accum=sc[:, 1 * 16 + tt: 1 * 16 + tt + 1], r=[], w=["junk", "ssk", "pbT"])
                                else:
                                    P.ts("dve", vb[:, tt * 128:(tt + 1) * 128], src, beta[:, tt * 8 + h: tt * 8 + h + 1], None, ALU.mult,
                                         r=["beta"], w=[f"vb{tt}", "pbT"])
                    P.act(sc[:, 0:32], sc[:, 0:32], AF.Sqrt, bias=1e-6, r=["ssq", "ssk"], w=["scA"])
                    P.op("dve", lambda e, o_=sc[:, 0:32], i_=sc[:, 0:32]: e.reciprocal(o_, i_), r=[], w=["scA"])
                    P.tt("dve", sc[:, 32:48], sc[:, 16:32], betav[:, :, h], ALU.mult, r=["scA", "beta"], w=["scB"])
                    P.tt("dve", sc[:, 32:48], sc[:, 32:48], egcv[:, :, h], ALU.mult, r=["egc"], w=["scB"])
                    P.tt("dve", sc[:, 48:64], sc[:, 16:32], egdv[:, :, h], ALU.mult, r=["scA", "egd"], w=["scB"])
                    P.ts("dve", sc[:, 64:80], sc[:, 0:16], DKS, None, ALU.mult, r=["scA"], w=["scB"])
                    P.tt("dve", sc[:, 80:96], sc[:, 64:80], egcv[:, :, h], ALU.mult, r=["egc"], w=["scB"])
                    P.ts("dve", sc[:, 96:112], betav[:, :, h], -1.0, None, ALU.mult, r=["beta"], w=["scB"])
                    for tt in range(NT):
                        ts_ = slice(tt * 128, (tt + 1) * 128)
                        P.ts("pool", kbg[:, ts_], k_tm[:, ts_], sc[:, 32 + tt:33 + tt], None, ALU.mult, r=["k_tm", "scB"], w=[f"kbg{tt}"])
                        P.ts("pool", kdec[:, ts_], k_tm[:, ts_], sc[:, 48 + tt:49 + tt], None, ALU.mult, r=["k_tm", "scB"], w=[f"kdec{tt}"])
                    for (src_tm, skey, scol, dstT, dkey) in ((k_tm, "k_tm", 16, knT, "knT"), (q_tm, "q_tm", 64, qnT, "qnT"), (q_tm, "q_tm", 80, qgT, "qgT")):
                        for tq in range(4):
                            for j in range(4):
                                tt = tq * 4 + j
                                P.ts("pool", tmq[:, j * 128:(j + 1) * 128], src_tm[:, tt * 128:(tt + 1) * 128], sc[:, scol + tt: scol + tt + 1], None,
                                     ALU.mult, r=[skey, "scA", "scB"], w=["tmq"])
                            for j in range(4):
                                P.tr(C.pbT[:, j * 128:(j + 1) * 128], tmq[:, j * 128:(j + 1) * 128], C.identb[:, :], r=["tmq", "identb"], w=["pbT"])
                            P.copy("act" if tq % 2 else "dve", dstT[:, tq * 512:(tq + 1) * 512], C.pbT[:, 0:512], r=[], w=[dkey, "pbT"])
                    def prep(tt):
                        sl = tt % 2
                        col = tt * 8 + h
                        tsl = slice(tt * 128, (tt + 1) * 128)
                        P.ts("pool", gtri[sl][:, :], C.triu, gp[:, col:col + 1], None, ALU.mult, r=["cst", "gp"], w=[f"gtri{sl}"])
                        b1 = next_bank(C)
                        P.mm(C.pb[b1][:, 0:128], C.ones, gtri[sl][:, :], r=["cst", f"gtri{sl}"], w=[f"pb{b1}"])
                        P.ts("dve", EL[sl][:, :], C.pb[b1][:, 0:128], Gcol[:, col:col + 1], 0.0, ALU.subtract, ALU.min, r=["Gcol"], w=[f"EL{sl}", f"pb{b1}"])
                        P.ts("dve", EU[sl][:, :], C.pb[b1][:, 0:128], Gcol[:, col:col + 1], 0.0, ALU.subtract, ALU.max, r=["Gcol"], w=[f"EU{sl}", f"pb{b1}"])
                        P.act(EL[sl][:, :], EL[sl][:, :], AF.Exp, scale=1.0, r=[], w=[f"EL{sl}"])
                        P.act(EU[sl][:, :], EU[sl][:, :], AF.Exp, scale=-1.0, r=[], w=[f"EU{sl}"])
                        P.tt("pool", EL[sl][:, :], EL[sl][:, :], C.trilS, ALU.mult, r=["cst"], w=[f"EL{sl}"])
                        P.tt("pool", EU[sl][:, :], EU[sl][:, :], C.triu, ALU.mult, r=["cst"], w=[f"EU{sl}"])
                        b2 = next_bank(C)
                        P.mm(C.pb[b2][:, 0:128], knT[:, tsl], knT[:, tsl], r=["knT"], w=[f"pb{b2}"])
                        P.mm(C.pb[b2][:, 128:256], knT[:, tsl], qnT[:, tsl], r=["knT", "qnT"], w=[f"pb{b2}"])
                        P.stt("dve", RR[sl][:, :], C.pb[b2][:, 0:128], sc[:, 96 + tt:97 + tt], EL[sl][:, :], ALU.mult, ALU.mult,
                              r=["scB", f"EL{sl}"], w=[f"RR{sl}", f"pb{b2}"])
                        P.tt("dve", Aq_sb[:, tsl], C.pb[b2][:, 128:256], EU[sl][:, :], ALU.mult, r=[f"EU{sl}"], w=[f"Aq{tt}", f"pb{b2}"])
                        b3 = next_bank(C)
                        P.tr(C.pb[b3][:, 0:128], RR[sl][:, :], C.ident, r=[f"RR{sl}", "cst"], w=[f"pb{b3}"])
                        P.copy("act", QX[sl][:, 0:128], C.pb[b3][:, 0:128], r=[], w=[f"QX{sl}", f"pb{b3}"])
                        P.tt("pool", QX[sl][:, 128:256], QX[sl][:, 0:128], C.ident, ALU.add, r=["cst"], w=[f"QX{sl}"])
                        for j in range(7):
                            bq = next_bank(C)
                            br = next_bank(C)
                            if j == 0:
                                P.mm(C.pb[bq][:, 0:128], RR[sl][:, :], QX[sl][:, 0:128], r=[f"RR{sl}", f"QX{sl}"], w=[f"pb{bq}"])
                            elif j <= 4:
                                P.mm(C.pb[bq][:, 0:256], RR[sl][:, :], QX[sl][:, 0:256], r=[f"RR{sl}", f"QX{sl}"], w=[f"pb{bq}"])
                            else:
                                P.mm(C.pb[bq][:, 128:256], RR[sl][:, :], QX[sl][:, 128:256], r=[f"RR{sl}", f"QX{sl}"], w=[f"pb{bq}"])
                            if j <= 5:
                                P.mm(C.pb[br][:, 0:128], QX[sl][:, 0:128], RR[sl][:, :], r=[f"RR{sl}", f"QX{sl}"], w=[f"pb{br}"])
                            if j <= 4:
                                P.copy("act", QX[sl][:, 0:128], C.pb[bq][:, 0:128], r=[], w=[f"QX{sl}", f"pb{bq}"])
                            if 1 <= j <= 5:
                                P.tt("dve", QX[sl][:, 128:256], C.pb[bq][:, 128:256], QX[sl][:, 128:256], ALU.add, r=[], w=[f"QX{sl}", f"pb{bq}"])
                            if j == 6:
                                P.tt("dve", XTb[sl][:, :], C.pb[bq][:, 128:256], QX[sl][:, 128:256], ALU.add, r=[f"QX{sl}"], w=[f"XTb{sl}", f"pb{bq}"])
                            if j <= 5:
                                P.copy("dve" if j % 2 else "act", RR[sl][:, :], C.pb[br][:, 0:128], r=[], w=[f"RR{sl}", f"pb{br}"])
                        b4 = next_bank(C)
                        P.mm(C.pb[b4][:, 0:128], XTb[sl][:, :], vb[:, tsl], r=[f"XTb{sl}", f"vb{tt}"], w=[f"pb{b4}"])
                        P.mm(C.pb[b4][:, 128:256], kbg[:, tsl], XTb[sl][:, :], r=[f"XTb{sl}", f"kbg{tt}"], w=[f"pb{b4}"])
                        P.copy("act", u_sb[:, tsl], C.pb[b4][:, 0:128], r=[], w=[f"u{tt}", f"pb{b4}"])
                        P.copy("dve", wT_sb[:, tsl], C.pb[b4][:, 128:256], r=[], w=[f"wT{tt}", f"pb{b4}"])

                    def scan(tt):
                        sl = tt % 2
                        col = tt * 8 + h
                        tsl = slice(tt * 128, (tt + 1) * 128)
                        b1 = next_bank(C)
                        P.mm(C.pb[b1][:, 0:128], wT_sb[:, tsl], Sbf[:, :], r=[f"wT{tt}", "Sbf"], w=[f"pb{b1}"])
                        P.tt("dve", vnew[sl][:, :], u_sb[:, tsl], C.pb[b1][:, 0:128], ALU.subtract, r=[f"u{tt}"], w=[f"vnew{sl}", f"pb{b1}"])
                        b2 = next_bank(C)
                        P.mm(C.pb[b2][:, 0:128], qgT[:, tsl], Sbf[:, :], start=True, stop=False, r=["qgT", "Sbf"], w=[f"pb{b2}"])
                        P.mm(C.pb[b2][:, 0:128], Aq_sb[:, tsl], vnew[sl][:, :], start=False, stop=True, r=[f"Aq{tt}", f"vnew{sl}"], w=[f"pb{b2}"])
                        b3 = next_bank(C)
                        P.mm(C.pb[b3][:, 0:128], kdec[:, tsl], vnew[sl][:, :], r=[f"kdec{tt}", f"vnew{sl}"], w=[f"pb{b3}"])
                        P.copy("dve", u_sb[:, tsl], C.pb[b2][:, 0:128], r=[], w=[f"u{tt}", f"pb{b2}"])
                        P.act(junk[:, :], C.pb[b2][:, 0:128], AF.Square, accum=sc[:, 112 + tt:113 + tt], r=[], w=["junk", "sso", f"pb{b2}"])
                        P.stt("dve", Sst[:, :], Sst[:, :], egl[:, col:col + 1], C.pb[b3][:, 0:128], ALU.mult, ALU.add, r=["egl"], w=["Sst", f"pb{b3}"])
                        P.copy("pool", Sbf[:, :], Sst[:, :], r=["Sst"], w=["Sbf"])

                    P.memset("pool", Sst[:, :], 0.0, w=["Sst"])
                    P.memset("pool", Sbf[:, :], 0.0, w=["Sbf"])
                    prep(0)
                    for tt in range(NT):
                        if tt + 1 < NT:
                            prep(tt + 1)
                        scan(tt)
                    P.act(sc[:, 128:144], sc[:, 112:128], AF.Sqrt, bias=1e-6, scale=1.0 / 128.0, r=["sso"], w=["scC"])
                    P.op("dve", lambda e, o_=sc[:, 128:144], i_=sc[:, 128:144]: e.reciprocal(o_, i_), r=[], w=["scC"])
                    for tq in range(4):
                        for j in range(4):
                            tt = tq * 4 + j
                            sl = tt % 2
                            tsl = slice(tt * 128, (tt + 1) * 128)
                            P.stt("dve", tno[sl][:, :], u_sb[:, tsl], sc[:, 128 + tt:129 + tt], nwB[:, :], ALU.mult, ALU.mult,
                                  r=[f"u{tt}", "scC", "nwB"], w=[f"tno{sl}"])
                            P.tt("pool", onb[sl][:, :], tno[sl][:, :], zs[:, tsl], ALU.mult, r=[f"tno{sl}", "zs"], w=[f"on1b{sl}"])
                            P.tr(C.pbT[:, j * 128:(j + 1) * 128], onb[sl][:, :], C.identb[:, :], r=[f"on1b{sl}", "identb"], w=["pbT"])
                        P.copy("act" if tq % 2 else "dve", oT[:, h * 2048 + tq * 512: h * 2048 + (tq + 1) * 512], C.pbT[:, 0:512],
                               r=[], w=[f"oT{h}", "pbT"])
            P.barrier()
            with ExitStack() as os_:
                wbig = P.sb("wbig1", [128, 8 * 1024], BF16, os_)
                load_w_bf16(C, D["gdn_w_out"], 0, 0, 1024, wbig, 0, "wbig")
                out_proj_ln(C, 1, b, oT, wbig, xin, xout, row0, tin, tout)
            P.barrier()


def make_consts():
    cst = np.zeros((128, 512), np.float32)
    cst[:, 0:128] = np.eye(128)
    cst[:, 128:256] = np.triu(np.ones((128, 128)))
    cst[:, 256:384] = np.tril(np.ones((128, 128)), -1)
    cst[:, 384:512] = 1.0
    oh4 = np.kron(np.eye(4, dtype=np.float32), np.ones((1, 128), np.float32))
    oh12 = np.kron(np.eye(12, dtype=np.float32), np.ones((1, 4), np.float32))
    return cst, oh4, oh12


def build(nseq, mode):
    nc = bass.Bass("TRN2", target_bir_lowering=False)
    D = {}

    def din(name, shape):
        D[name] = nc.dram_tensor(name, list(shape), F32, kind="ExternalInput").ap()

    din("x", [nseq * S_LEN, DM])
    din("c4", [4, DM])
    din("ada_w", [2 * DM, 3 * DM])
    din("ada_b", [12, 512])
    din("oh12", [12, 48])
    din("ln_g", [2, DM])
    din("ln_b", [2, DM])
    din("moba_w_in", [DM, 4 * DM])
    din("moba_w_out", [DM, DM])
    din("gdn_w_in", [DM, 4112])
    din("gdn_conv_w", [4, 3072])
    din("gdn_a_log", [1, 8])
    din("gdn_dt_bias", [1, 8])
    din("gdn_norm_w", [1, 128])
    din("gdn_w_out", [DM, DM])
    din("cst", [128, 512])
    din("oh4", [4, 512])
    out = nc.dram_tensor("out", [nseq * S_LEN, DM], F32, kind="ExternalOutput").ap()
    with ExitStack() as st:
        P = Prog(nc, st)
        C = setup_common(P, D)
        P.barrier()
        if mode == "l0":
            layer0(C, nseq, D["x"], out)
        elif mode == "l1":
            layer1(C, nseq, D["x"], out)
        else:
            x1 = nc.dram_tensor("x1s", [nseq * S_LEN, DM], F32).ap()
            layer0(C, nseq, D["x"], x1, "x", "x1")
            P.barrier()
            layer1(C, nseq, x1, out, "x1", "out")
        stats = P.emit()
    return nc, stats


NCORES = 8
NSEQ = 4


def kernel(**inputs):
    f = lambda k: np.ascontiguousarray(np.asarray(inputs[k], dtype=np.float32))
    x = f("x")
    c = f("c")
    B = x.shape[0]
    assert B == NCORES * NSEQ and x.shape[1] == S_LEN and x.shape[2] == DM
    nc, _ = build(NSEQ, "full")
    cst, oh4, oh12 = make_consts()
    shared = {
        "ada_w": f("ada_w").reshape(2 * DM, 3 * DM),
        "ada_b": f("ada_b").reshape(12, 512),
        "oh12": oh12,
        "ln_g": f("ln_g"),
        "ln_b": f("ln_b"),
        "moba_w_in": f("moba_w_in")[0],
        "moba_w_out": f("moba_w_out")[0],
        "gdn_w_in": f("gdn_w_in")[0],
        "gdn_conv_w": f("gdn_conv_w")[0],
        "gdn_a_log": f("gdn_a_log"),
        "gdn_dt_bias": f("gdn_dt_bias"),
        "gdn_norm_w": f("gdn_norm_w"),
        "gdn_w_out": f("gdn_w_out")[0],
        "cst": cst,
        "oh4": oh4,
    }
    in_maps = []
    for core in range(NCORES):
        b0 = core * NSEQ
        m = dict(shared)
        m["x"] = x[b0:b0 + NSEQ].reshape(NSEQ * S_LEN, DM)
        m["c4"] = c[b0:b0 + NSEQ]
        in_maps.append(m)
    res = run_bass_kernel_spmd(nc, in_maps, core_ids=list(range(NCORES)))
    out = np.concatenate([np.asarray(r["out"]).reshape(NSEQ, S_LEN, DM) for r in res.results], axis=0)
    return out.astype(np.float32)
```

```python
import numpy as np
from contextlib import ExitStack
import concourse.bass as bass
import concourse.mybir as mybir
from concourse.bass_utils import run_bass_kernel_spmd

F32 = mybir.dt.float32
BF16 = mybir.dt.bfloat16
AF = mybir.ActivationFunctionType
ALU = mybir.AluOpType
AX = mybir.AxisListType

S_LEN = 2048
DM = 1024
NT = 16
ALPHA = 4.0 ** 0.25
LN_EPS = 1e-5
NEG = -1.0e30

EPOCH = 30000
NSEM_PER_ENG = 8


class Op:
    __slots__ = ("eng", "fn", "deps", "needed", "sem", "val", "dma_key", "gidx")


class Prog:
    ENGS = ("pe", "act", "dve", "pool", "sp")

    def __init__(self, nc, stack, same_sync=True):
        self.nc = nc
        self.stack = stack
        self.same_sync = same_sync
        self.streams = {e: [] for e in self.ENGS}
        self.last_w = {}
        self.readers = {}
        self.dma_sems = {}
        self.eng_sems = {}
        self.consts = set()
        for e in ("pe", "act", "dve", "pool"):
            self.eng_sems[e] = [stack.enter_context(nc.semaphore(f"s_{e}{i}")) for i in range(NSEM_PER_ENG)]
        self.out_dmas = []
        self.nbank = 0

    def sb(self, name, shape, dt, stack=None):
        self.nsb = getattr(self, "nsb", 0) + 1
        return (stack or self.stack).enter_context(self.nc.sbuf_tensor(f"sb{self.nsb}_{name}", list(shape), dt))

    def ps(self, name, shape, dt=F32):
        return self.stack.enter_context(self.nc.psum_tensor(name, list(shape), dt))

    def op(self, eng, fn, r=(), w=(), dma_key=None):
        o = Op()
        o.eng = eng
        o.fn = fn
        o.needed = False
        o.dma_key = dma_key
        o.deps = []
        o.sem = None
        o.val = 0
        o.gidx = 0
        deps = []
        for k in r:
            lw = self.last_w.get(k)
            if lw is not None:
                deps.append(lw)
        for k in w:
            lw = self.last_w.get(k)
            if lw is not None:
                deps.append(lw)
            deps.extend(self.readers.get(k, ()))
        pend = getattr(self, "pending", None)
        if pend and pend.get(eng):
            deps.extend(pend[eng])
            pend[eng] = []
        seen = set()
        for d in deps:
            if id(d) in seen:
                continue
            seen.add(id(d))
            if d.dma_key is None and d.eng == eng and (eng == "pe" or not self.same_sync):
                continue
            d.needed = True
            o.deps.append(d)
        for k in w:
            self.last_w[k] = o
            self.readers[k] = []
        for k in r:
            if k in w or k in self.consts:
                continue
            self.readers.setdefault(k, []).append(o)
        if dma_key is not None:
            ent = self.dma_sems.get(dma_key)
            if ent is None:
                ent = [self.stack.enter_context(self.nc.semaphore(f"d_{len(self.dma_sems)}")), 0]
                self.dma_sems[dma_key] = ent
            ent[1] += 1
            o.sem = ent[0]
            o.val = 16 * ent[1]
            o.needed = True
        self.streams[eng].append(o)
        return o

    def barrier(self):
        lasts = []
        for e in ("pe", "act", "dve", "pool"):
            for o in reversed(self.streams[e]):
                if o.dma_key is None and o.fn is not None:
                    lasts.append(o)
                    break
        self.pending = {e: list(lasts) for e in self.ENGS}

    def mark_const(self, *keys):
        for k in keys:
            self.consts.add(k)
            self.readers.pop(k, None)

    def dma(self, out, in_, r=(), w=(), key=None, eng="sp", is_out=False):
        o = self.op(eng, lambda e: e.dma_start(out=out, in_=in_), r=r, w=w, dma_key=key)
        if is_out:
            self.out_dmas.append(o)
        return o

    def mm(self, out, lhsT, rhs, start=True, stop=True, r=(), w=()):
        return self.op("pe", lambda e: e.matmul(out, lhsT, rhs, start=start, stop=stop), r=r, w=w)

    def tr(self, out, in_, ident, r=(), w=()):
        return self.op("pe", lambda e: e.transpose(out, in_, ident), r=r, w=w)

    def act(self, out, in_, func, bias=0.0, scale=1.0, accum=None, r=(), w=()):
        if accum is None:
            return self.op("act", lambda e: e.activation(out, in_, func, bias=bias, scale=scale), r=r, w=w)
        return self.op("act", lambda e: e.activation(out, in_, func, bias=bias, scale=scale, accum_out=accum), r=r, w=w)

    def ts(self, eng, out, in0, s1, s2, op0, op1=None, r=(), w=()):
        if op1 is None:
            return self.op(eng, lambda e: e.tensor_scalar(out, in0, s1, None, op0=op0), r=r, w=w)
        return self.op(eng, lambda e: e.tensor_scalar(out, in0, s1, s2, op0=op0, op1=op1), r=r, w=w)

    def tt(self, eng, out, in0, in1, op, r=(), w=()):
        return self.op(eng, lambda e: e.tensor_tensor(out, in0, in1, op=op), r=r, w=w)

    def stt(self, eng, out, in0, scalar, in1, op0, op1, r=(), w=()):
        return self.op(eng, lambda e: e.scalar_tensor_tensor(out, in0, scalar, in1, op0=op0, op1=op1), r=r, w=w)

    def copy(self, eng, out, in_, r=(), w=()):
        if eng == "act":
            return self.op("act", lambda e: e.copy(out, in_), r=r, w=w)
        return self.op(eng, lambda e: e.tensor_copy(out, in_), r=r, w=w)

    def memset(self, eng, ap, val, w=()):
        return self.op(eng, lambda e: e.memset(ap, val), w=w)

    def emit(self):
        nc = self.nc
        for e in self.ENGS:
            cnt = 0
            for o in self.streams[e]:
                if o.dma_key is None and o.needed:
                    assert e != "sp"
                    o.gidx = cnt
                    assert cnt // EPOCH < NSEM_PER_ENG, "too many signalling ops"
                    o.sem = self.eng_sems[e][cnt // EPOCH]
                    o.val = cnt % EPOCH + 1
                    cnt += 1
        fin = Op()
        fin.eng = "sp"
        fin.fn = None
        fin.deps = list(self.out_dmas)
        fin.needed = False
        fin.dma_key = None
        fin.sem = None
        fin.val = 0
        fin.gidx = 0
        self.streams["sp"].append(fin)
        stats = {}
        with nc.Block() as block:

            @block.tensor
            def _(eng):
                stats["pe"] = self._emit_stream("pe", eng)

            @block.scalar
            def _(eng):
                stats["act"] = self._emit_stream("act", eng)

            @block.vector
            def _(eng):
                stats["dve"] = self._emit_stream("dve", eng)

            @block.gpsimd
            def _(eng):
                stats["pool"] = self._emit_stream("pool", eng)

            @block.sync
            def _(eng):
                stats["sp"] = self._emit_stream("sp", eng)

        return stats

    def _emit_stream(self, e, eng):
        waited = {}
        nwait = 0
        nops = 0
        for o in self.streams[e]:
            for d in o.deps:
                if d.dma_key is not None:
                    k = ("d", d.dma_key)
                    v = d.val
                else:
                    k = ("e", d.eng)
                    v = d.gidx + 1
                if waited.get(k, 0) >= v:
                    continue
                eng.wait_ge(d.sem, d.val)
                waited[k] = v
                nwait += 1
            if o.fn is None:
                continue
            ins = o.fn(eng)
            nops += 1
            if o.dma_key is not None:
                ins.then_inc(o.sem, 16)
            elif o.needed:
                ins.then_inc(o.sem, 1)
        return (nops, nwait)


class Ctx:
    pass


def next_bank(C):
    b = C.bank_rr % 7
    C.bank_rr += 1
    return b


def next_stg(C):
    si = C.stg_rr % C.NSTG
    C.stg_rr += 1
    return si


def SK(si):
    return [f"stg{si}.{j}" for j in range(8)]


def setup_common(P, D):
    C = Ctx()
    C.P = P
    C.D = D
    C.pb = [P.ps(f"pb{i}", [128, 512], F32) for i in range(7)]
    C.pbT = P.ps("pbT", [128, 1024], BF16)
    C.bank_rr = 0
    C.NSTG = 5
    C.stg = [P.sb(f"stg{i}", [128, 1024], F32) for i in range(C.NSTG)]
    C.stg_rr = 0

    cst = P.sb("cst", [128, 768], F32)
    P.dma(cst[:], D["cst"][:, :], w=["cst"], key="cst")
    P.mark_const("cst")
    C.cst = cst
    C.ident = cst[:, 0:128]
    C.triu = cst[:, 128:256]
    C.trilS = cst[:, 256:384]
    C.ones = cst[:, 384:512]
    C.negL = cst[:, 512:640]
    C.posU = cst[:, 640:768]
    C.identb = P.sb("identb", [128, 128], BF16)
    C.triub = P.sb("triub", [128, 128], BF16)
    P.copy("pool", C.identb[:], C.ident, r=["cst"], w=["identb"])
    P.copy("pool", C.triub[:], C.triu, r=["cst"], w=["triub"])
    P.mark_const("identb", "triub")
    oh4 = P.sb("oh4", [4, 512], F32)
    P.dma(oh4[:], D["oh4"][:, :], w=["oh4"], key="oh4")
    P.mark_const("oh4")
    C.oh4 = oh4

    modg = P.sb("modg", [4, 2048], F32)
    C.ssT = P.sb("ssT", [128, 128], F32)
    C.lnB = P.sb("lnB", [128, 4096], F32)
    bk = C.pb[6]
    with ExitStack() as ts_:
        c_sb = P.sb("c_sb", [4, 1024], F32, ts_)
        cs_sb = P.sb("cs_sb", [4, 1024], F32, ts_)
        csT = P.sb("csT", [128, 32], F32, ts_)
        adab = P.sb("adab", [12, 512], F32, ts_)
        oh12 = P.sb("oh12", [12, 48], F32, ts_)
        modss = P.sb("modss", [4, 2048], F32, ts_)
        lnrow = P.sb("lnrow", [4, 1024], F32, ts_)
        P.dma(c_sb[:], D["c4"][:, :], w=["c_sb"], key="c_sb")
        P.dma(adab[:], D["ada_b"][:, :], w=["adab"], key="adab")
        P.dma(oh12[:], D["oh12"][:, :], w=["oh12"], key="oh12")
        P.dma(lnrow[0:2, :], D["ln_g"][:, :], w=["lnrow0"], key="lnrow0")
        P.dma(lnrow[2:4, :], D["ln_b"][:, :], w=["lnrow1"], key="lnrow1")
        P.act(cs_sb[:], c_sb[:], AF.Silu, r=["c_sb"], w=["cs_sb"])
        for c in range(8):
            P.tr(bk[:, c * 4:(c + 1) * 4], cs_sb[0:4, c * 128:(c + 1) * 128], cst[0:4, 0:4], r=["cs_sb", "cst"], w=["pb6"])
        P.copy("dve", csT[:], bk[:, 0:32], r=[], w=["csT", "pb6"])
        for l in range(2):
            for c in range(8):
                for th in range(3):
                    si = next_stg(C)
                    P.dma(C.stg[si][:], D["ada_w"][l * 1024 + c * 128: l * 1024 + (c + 1) * 128, th * 1024:(th + 1) * 1024],
                          w=SK(si), key=f"stg{si}")
                    for hf in range(2):
                        b = th * 2 + hf
                        P.mm(C.pb[b][0:4, :], csT[:, c * 4:(c + 1) * 4], C.stg[si][:, hf * 512:(hf + 1) * 512],
                             start=(c == 0), stop=False, r=SK(si) + ["csT"], w=[f"pb{b}"])
            for b in range(6):
                k = l * 6 + b
                P.mm(C.pb[b][0:4, :], oh12[:, k * 4:(k + 1) * 4], adab[:, :],
                     start=False, stop=True, r=["adab", "oh12"], w=[f"pb{b}"])
            for b in range(4):
                P.copy("dve" if b % 2 else "act", modss[:, b * 512:(b + 1) * 512], C.pb[b][0:4, :], r=[], w=["modss", f"pb{b}"])
            for b in range(4, 6):
                P.copy("dve" if b % 2 else "act", modg[:, l * 1024 + (b - 4) * 512: l * 1024 + (b - 3) * 512], C.pb[b][0:4, :],
                       r=[], w=["modg", f"pb{b}"])
            for k in range(16):
                P.tr(bk[:, k * 4:(k + 1) * 4], modss[0:4, k * 128:(k + 1) * 128], cst[0:4, 0:4], r=["modss", "cst"], w=["pb6"])
            P.copy("dve", C.ssT[:, l * 64: l * 64 + 32], bk[:, 0:32], r=[], w=["ssT", "pb6"])
            P.ts("dve", C.ssT[:, l * 64 + 32: l * 64 + 64], bk[:, 32:64], 1.0, None, ALU.add, r=[], w=["ssT", "pb6"])
        for k in range(8):
            b = k % 6
            rr = k // 2
            hf = k % 2
            P.mm(C.pb[b][:, :], oh4[0:4, rr * 128:(rr + 1) * 128], lnrow[0:4, hf * 512:(hf + 1) * 512],
                 r=["lnrow0", "lnrow1", "oh4"], w=[f"pb{b}"])
            P.copy("dve" if k % 2 else "act", C.lnB[:, k * 512:(k + 1) * 512], C.pb[b][:, :], r=[], w=["lnB", f"pb{b}"])
        C.prologue_keys = ["c_sb", "cs_sb", "csT", "adab", "oh12", "modss", "lnrow0", "lnrow1"]
    C.modg = modg
    C.gateB = P.sb("gateB", [128, 1024], F32)
    return C


def make_gateB(C, l, b):
    P = C.P
    for hf in range(2):
        bk = next_bank(C)
        P.mm(C.pb[bk][:, :], C.oh4[0:4, b * 128:(b + 1) * 128], C.modg[0:4, l * 1024 + hf * 512: l * 1024 + (hf + 1) * 512],
             r=["oh4", "modg"], w=[f"pb{bk}"])
        P.copy("act", C.gateB[:, hf * 512:(hf + 1) * 512], C.pb[bk][:, :], r=[], w=["gateB", f"pb{bk}"])


def build_hT(C, l, b, xin, row0, hT, tin="x"):
    P = C.P
    for tg in range(4):
        sis = []
        for j in range(4):
            t = tg * 4 + j
            si = next_stg(C)
            sis.append(si)
            P.dma(C.stg[si][:], xin[row0 + t * 128: row0 + (t + 1) * 128, :], r=[f"X{tin}.{row0 + t * 128}"], w=SK(si), key=f"stg{si}")
        for c in range(8):
            bk = next_bank(C)
            for j in range(4):
                P.tr(C.pb[bk][:, j * 128:(j + 1) * 128], C.stg[sis[j]][:, c * 128:(c + 1) * 128], C.ident,
                     r=SK(sis[j]) + ["cst"], w=[f"pb{bk}"])
            sh = C.ssT[:, l * 64 + c * 4 + b: l * 64 + c * 4 + b + 1]
            sc = C.ssT[:, l * 64 + 32 + c * 4 + b: l * 64 + 32 + c * 4 + b + 1]
            dst = hT[:, c * 2048 + tg * 512: c * 2048 + (tg + 1) * 512]
            if c % 2 == 0:
                P.ts("dve", dst, C.pb[bk][:, :], sc, sh, ALU.mult, ALU.add, r=["ssT"], w=[f"hT{c}.{tg}", f"pb{bk}"])
            else:
                P.act(dst, C.pb[bk][:, :], AF.Identity, bias=sh, scale=sc, r=["ssT"], w=[f"hT{c}.{tg}", f"pb{bk}"])


def load_w_bf16(C, wsrc, rows0, col0, ncols, dst, dst_off, dkey, nchunk=8, cast_eng="pool"):
    P = C.P
    per = 1024 // ncols
    c = 0
    while c < nchunk:
        n = min(per, nchunk - c)
        si = next_stg(C)
        sk = f"stg{si}"
        for j in range(n):
            P.dma(C.stg[si][:, j * ncols:(j + 1) * ncols],
                  wsrc[rows0 + (c + j) * 128: rows0 + (c + j + 1) * 128, col0:col0 + ncols],
                  w=(SK(si) if n == 1 else [f"stg{si}.{j}"]), key=sk)
        P.copy(cast_eng, dst[:, dst_off + c * ncols: dst_off + (c + n) * ncols], C.stg[si][:, 0:n * ncols], r=SK(si), w=[dkey])
        c += n


def out_proj_ln(C, l, b, oT, wbig, xres, xout, row0, tin="x", tout="out"):
    P = C.P
    for tt in range(NT):
        bks = [next_bank(C), next_bank(C)]
        for hf in range(2):
            for h in range(8):
                P.mm(C.pb[bks[hf]][:, :], oT[:, h * 2048 + tt * 128: h * 2048 + (tt + 1) * 128],
                     wbig[:, h * 1024 + hf * 512: h * 1024 + (hf + 1) * 512], start=(h == 0), stop=(h == 7),
                     r=[f"oT{h}", "wbig"], w=[f"pb{bks[hf]}"])
        sx = next_stg(C)
        s1 = next_stg(C)
        P.dma(C.stg[sx][:], xres[row0 + tt * 128: row0 + (tt + 1) * 128, :], r=[f"X{tin}.{row0 + tt * 128}"], w=SK(sx), key=f"stg{sx}")
        t1 = C.stg[s1]
        for hf in range(2):
            P.tt("dve", t1[:, hf * 512:(hf + 1) * 512], C.pb[bks[hf]][:, :], C.gateB[:, hf * 512:(hf + 1) * 512], ALU.mult,
                 r=["gateB"], w=SK(s1) + [f"pb{bks[hf]}"])
        P.stt("dve", t1[:], C.stg[sx][:], ALPHA, t1[:], ALU.mult, ALU.add, r=SK(sx), w=SK(s1))
        st = C.lnst
        P.op("dve", lambda e, t1=t1, st=st: e.bn_stats(st[:, 0:6], t1[:, 0:512]), r=SK(s1), w=["lnst"])
        P.op("dve", lambda e, t1=t1, st=st: e.bn_stats(st[:, 6:12], t1[:, 512:1024]), r=SK(s1), w=["lnst"])
        P.op("dve", lambda e, st=st: e.bn_aggr(st[:, 12:14], st[:, 0:12].rearrange("p (a b) -> p a b", a=2)), r=[], w=["lnst"])
        P.act(st[:, 14:15], st[:, 13:14], AF.Sqrt, bias=LN_EPS, scale=1.0, r=[], w=["lnst"])
        P.op("dve", lambda e, st=st: e.reciprocal(st[:, 15:16], st[:, 14:15]), r=[], w=["lnst"])
        P.stt("dve", st[:, 16:17], st[:, 12:13], -1.0, st[:, 15:16], ALU.mult, ALU.mult, r=[], w=["lnst"])
        xn = C.stg[sx]
        P.act(xn[:], t1[:], AF.Identity, bias=st[:, 16:17], scale=st[:, 15:16], r=SK(s1) + ["lnst"], w=SK(sx))
        P.tt("pool", xn[:], xn[:], C.lnB[:, l * 1024:(l + 1) * 1024], ALU.mult, r=["lnB"], w=SK(sx))
        P.tt("pool", xn[:], xn[:], C.lnB[:, 2048 + l * 1024: 2048 + (l + 1) * 1024], ALU.add, r=["lnB"], w=SK(sx))
        P.dma(xout[row0 + tt * 128: row0 + (tt + 1) * 128, :], xn[:], r=SK(sx), w=[f"X{tout}.{row0 + tt * 128}"], key=f"stg{sx}", is_out=True)


def layer0(C, nseq, xin, xout, tin="x", tout="out"):
    P = C.P
    D = C.D
    with ExitStack() as ls:
        hT = P.sb("hT", [128, 8 * 2048], BF16, ls)
        vall = P.sb("vall", [128, NT * 8 * 129], BF16, ls)
        oT = P.sb("oT", [128, 8 * 2048], BF16, ls)
        wbig = P.sb("wbig", [128, 8 * 1024], BF16, ls)
        wh = [P.sb(f"wh{i}", [128, 8 * 384], BF16, ls) for i in range(2)]
        qT = P.sb("qT", [128, 2048], BF16, ls)
        kT = P.sb("kT", [128, 2048], BF16, ls)
        zT = P.sb("zT", [128, 2048], BF16, ls)
        kms = P.sb("kms", [128, 8], F32, ls)
        kmb = P.sb("kmb", [128, 8], BF16, ls)
        gm = P.sb("gm", [128, 128], F32, ls)
        top8 = P.sb("top8", [128, 8], F32, ls)
        msk = P.sb("msk", [128, 128], F32, ls)
        NPT = 4
        PT = [P.sb(f"PT{i}", [128, 512], BF16, ls) for i in range(NPT)]
        acc = P.sb("acc", [128, 2 * 129], F32, ls)
        rden = P.sb("rden", [128, 2], F32, ls)
        onb = [P.sb(f"onb{i}", [128, 128], BF16, ls) for i in range(2)]
        C.lnst = P.sb("lnst0", [128, 32], F32, ls)
        vv = vall[:, :].rearrange("p (t h e) -> p t h e", t=NT, h=8)
        P.memset("pool", vall[:, :], 1.0, w=[f"v{tt}" for tt in range(NT)])
        P.memset("pool", gm[:, :], NEG, w=["gm"])
        P.memset("pool", msk[:, :], 1.0, w=["msk"])
        scale = 128.0 ** -0.5
        pt_rr = 0
        for s in range(nseq):
            b = s
            row0 = s * S_LEN
            make_gateB(C, 0, b)
            build_hT(C, 0, b, xin, row0, hT, tin)
            hkeys = lambda tq: [f"hT{c}.{tq}" for c in range(8)]
            load_w_bf16(C, D["moba_w_in"], 0, 2048, 1024, wbig, 0, "wbig")
            for tt in range(NT):
                for hf in range(2):
                    bk = next_bank(C)
                    for c in range(8):
                        P.mm(C.pb[bk][:, :], hT[:, c * 2048 + tt * 128: c * 2048 + (tt + 1) * 128],
                             wbig[:, c * 1024 + hf * 512: c * 1024 + (hf + 1) * 512], start=(c == 0), stop=(c == 7),
                             r=[f"hT{c}.{tt // 4}", "wbig"], w=[f"pb{bk}"])
                    dst = vv[:, tt, hf * 4:(hf + 1) * 4, 0:128]
                    src = C.pb[bk][:, :].rearrange("p (h e) -> p h e", h=4)
                    P.copy("act" if hf else "dve", dst, src, r=[], w=[f"v{tt}", f"pb{bk}"])
            for h in range(8):
                whh = wh[h % 2]
                wk_ = f"wh{h % 2}"
                for mi, col0 in enumerate((h * 128, 1024 + h * 128, 3072 + h * 128)):
                    load_w_bf16(C, D["moba_w_in"], 0, col0, 128, whh, mi * 1024, wk_)
                for mi, (dstT, dk) in enumerate(((qT, "qT"), (kT, "kT"), (zT, "zT"))):
                    for tq in range(4):
                        bk = next_bank(C)
                        for c in range(8):
                            P.mm(C.pb[bk][:, :], whh[:, mi * 1024 + c * 128: mi * 1024 + (c + 1) * 128],
                                 hT[:, c * 2048 + tq * 512: c * 2048 + (tq + 1) * 512], start=(c == 0), stop=(c == 7),
                                 r=[wk_, f"hT{c}.{tq}"], w=[f"pb{bk}"])
                        dst = dstT[:, tq * 512:(tq + 1) * 512]
                        if mi == 0:
                            P.copy("act", dst, C.pb[bk][:, :], r=[], w=[f"qT{tq}", f"pb{bk}"])
                        elif mi == 1:
                            P.copy("dve", dst, C.pb[bk][:, :], r=[], w=[f"kT{tq}", f"pb{bk}"])
                            P.op("dve", lambda e, o_=kms[:, 2 * tq:2 * tq + 2], i_=C.pb[bk][:, :].rearrange("p (a b) -> p a b", a=2):
                                 e.tensor_reduce(o_, i_, axis=AX.X, op=ALU.add), r=[], w=["kms", f"pb{bk}"])
                        else:
                            P.act(dst, C.pb[bk][:, :], AF.Silu, r=[], w=[f"zT{tq}", f"pb{bk}"])
                P.copy("dve", kmb[:, :], kms[:, :], r=["kms"], w=["kmb"])
                bg = next_bank(C)
                for qt in range(8, NT):
                    P.mm(C.pb[bg][:, qt * 8:(qt + 1) * 8], qT[:, qt * 128:(qt + 1) * 128], kmb[:, :],
                         r=[f"qT{qt // 4}", "kmb"], w=[f"pb{bg}"])
                for i in range(4, 8):
                    src = C.pb[bg][:, i * 16:(i + 1) * 16].rearrange("p (a b) -> p a b", a=2)[:, :, 0:i]
                    dst = gm[:, i * 16:(i + 1) * 16].rearrange("p (a b) -> p a b", a=2)[:, :, 0:i]
                    P.copy("dve", dst, src, r=[], w=["gm", f"pb{bg}"])
                for qt in range(8, NT):
                    i = qt // 2
                    P.op("dve", lambda e, o_=top8[:, :], i_=gm[:, qt * 8:(qt + 1) * 8]: e.max(out=o_, in_=i_), r=["gm"], w=["top8"])
                    P.ts("dve", msk[:, qt * 8: qt * 8 + i], gm[:, qt * 8: qt * 8 + i], top8[:, 2:3], None, ALU.is_ge,
                         r=["gm", "top8"], w=["msk"])
                for i in range(8):
                    qa, qb = 2 * i, 2 * i + 1
                    qsl = slice(i * 256, (i + 1) * 256)
                    qk = f"qT{i // 2}"

                    def scores(n):
                        nonlocal pt_rr
                        bk = next_bank(C)
                        for j in range(2):
                            kt = 2 * n + j
                            P.mm(C.pb[bk][:, j * 256:(j + 1) * 256], kT[:, kt * 128:(kt + 1) * 128], qT[:, qsl],
                                 r=[f"kT{kt // 4}", qk], w=[f"pb{bk}"])
                        pi = pt_rr % NPT
                        pt_rr += 1
                        P.act(PT[pi][:, :], C.pb[bk][:, :], AF.Exp, scale=scale, r=[], w=[f"PT{pi}", f"pb{bk}"])
                        if n == i:
                            P.tt("pool", PT[pi][:, 0:128], PT[pi][:, 0:128], C.triub[:, :], ALU.mult, r=["triub"], w=[f"PT{pi}"])
                            P.tt("pool", PT[pi][:, 384:512], PT[pi][:, 384:512], C.triub[:, :], ALU.mult, r=["triub"], w=[f"PT{pi}"])
                        return pi

                    def pv(bko, pi, n, first, last):
                        for qi, qt in enumerate((qa, qb)):
                            kts = [0, 1]
                            if n == i and qi == 0:
                                kts = [0]
                            for j in kts:
                                kt = 2 * n + j
                                P.mm(C.pb[bko][:, qi * 256: qi * 256 + 129], PT[pi][:, j * 256 + qi * 128: j * 256 + (qi + 1) * 128],
                                     vv[:, kt, h, :], start=(first and j == 0), stop=(last and j == kts[-1]),
                                     r=[f"PT{pi}", f"v{kt}"], w=[f"pb{bko}"])

                    def finish(qi, qt, src_ap, src_keys_r, src_keys_w):
                        P.op("dve", lambda e, o_=rden[:, qi:qi + 1], i_=src_ap[:, 128:129]: e.reciprocal(o_, i_),
                             r=src_keys_r, w=["rden"] + src_keys_w)
                        ob = onb[qi]
                        P.ts("dve", ob[:, :], src_ap[:, 0:128], rden[:, qi:qi + 1], None, ALU.mult,
                             r=["rden"] + src_keys_r, w=[f"onb{qi}"] + src_keys_w)
                        P.tr(C.pbT[:, qi * 128:(qi + 1) * 128], ob[:, :], C.identb[:, :], r=[f"onb{qi}", "identb"], w=["pbT"])
                        P.tt("dve", oT[:, h * 2048 + qt * 128: h * 2048 + (qt + 1) * 128], C.pbT[:, qi * 128:(qi + 1) * 128],
                             zT[:, qt * 128:(qt + 1) * 128], ALU.mult, r=[f"zT{qt // 4}"], w=[f"oT{h}", "pbT"])

                    if i <= 3:
                        blocks = list(range(i, -1, -1))
                        pis = {}
                        bko = next_bank(C)
                        for n in blocks:
                            pis[n] = scores(n)
                        for qi, qt in enumerate((qa, qb)):
                            mms = []
                            for n in blocks:
                                kts = [0, 1]
                                if n == i and qi == 0:
                                    kts = [0]
                                for j in kts:
                                    mms.append((n, j))
                            for idx, (n, j) in enumerate(mms):
                                kt = 2 * n + j
                                pi = pis[n]
                                P.mm(C.pb[bko][:, qi * 256: qi * 256 + 129], PT[pi][:, j * 256 + qi * 128: j * 256 + (qi + 1) * 128],
                                     vv[:, kt, h, :], start=(idx == 0), stop=(idx == len(mms) - 1),
                                     r=[f"PT{pi}", f"v{kt}"], w=[f"pb{bko}"])
                        for qi, qt in enumerate((qa, qb)):
                            finish(qi, qt, C.pb[bko][:, qi * 256: qi * 256 + 129], [], [f"pb{bko}"])
                    else:
                        order = [i] + list(range(i))
                        pend = None
                        for idx, n in enumerate(order):
                            pi = scores(n)
                            bko = next_bank(C)
                            pv(bko, pi, n, True, True)
                            for qi, qt in enumerate((qa, qb)):
                                src = C.pb[bko][:, qi * 256: qi * 256 + 129]
                                dst = acc[:, qi * 129:(qi + 1) * 129]
                                if idx == 0:
                                    P.copy("dve", dst, src, r=[], w=[f"acc{qi}", f"pb{bko}"])
                                else:
                                    P.stt("dve", dst, src, msk[:, qt * 8 + n: qt * 8 + n + 1], dst, ALU.mult, ALU.add,
                                          r=["msk"], w=[f"acc{qi}", f"pb{bko}"])
                        for qi, qt in enumerate((qa, qb)):
                            finish(qi, qt, acc[:, qi * 129:(qi + 1) * 129], [f"acc{qi}"], [])
            load_w_bf16(C, D["moba_w_out"], 0, 0, 1024, wbig, 0, "wbig")
            out_proj_ln(C, 0, b, oT, wbig, xin, xout, row0, tin, tout)


def layer1(C, nseq, xin, xout, tin="x", tout="out"):
    P = C.P
    D = C.D
    W = D["gdn_w_in"]
    with ExitStack() as ls:
        hT = P.sb("hT1", [128, 8 * 2048], BF16, ls)
        oT = P.sb("oT1", [128, 8 * 2048], BF16, ls)
        C.lnst = P.sb("lnst1", [128, 32], F32, ls)
        convT = P.sb("convT", [128, 96], F32, ls)
        nwB = P.sb("nwB", [128, 128], F32, ls)
        ealogB = P.sb("ealogB", [128, 128], F32, ls)
        dtbB = P.sb("dtbB", [128, 128], F32, ls)
        gp = P.sb("gp", [128, 128], F32, ls)
        beta = P.sb("beta", [128, 128], F32, ls)
        Gcol = P.sb("Gcol", [128, 128], F32, ls)
        egc = P.sb("egc", [128, 128], F32, ls)
        egd = P.sb("egd", [128, 128], F32, ls)
        egl = P.sb("egl", [128, 128], F32, ls)
        with ExitStack() as ts_:
            cw = P.sb("cw", [4, 3072], F32, ts_)
            ab2 = P.sb("ab2", [2, 8], F32, ts_)
            nwr = P.sb("nwr", [1, 128], F32, ts_)
            P.dma(cw[:], D["gdn_conv_w"][:, :], w=["cw"], key="cw")
            P.dma(ab2[0:1, :], D["gdn_a_log"][:, :], w=["ab2a"], key="ab2a")
            P.dma(ab2[1:2, :], D["gdn_dt_bias"][:, :], w=["ab2b"], key="ab2b")
            P.dma(nwr[:], D["gdn_norm_w"][:, :], w=["nwr"], key="nwr")
            bk = next_bank(C)
            for g in range(24):
                P.tr(C.pb[bk][:, g * 4:(g + 1) * 4], cw[0:4, g * 128:(g + 1) * 128], C.cst[0:4, 0:4], r=["cw", "cst"], w=[f"pb{bk}"])
            P.copy("dve", convT[:, :], C.pb[bk][:, 0:96], r=[], w=["convT", f"pb{bk}"])
            bk = next_bank(C)
            P.mm(C.pb[bk][:, 0:8], C.oh4[0:2, 0:128], ab2[0:2, :], r=["ab2a", "ab2b", "oh4"], w=[f"pb{bk}"])
            P.mm(C.pb[bk][:, 8:16], C.oh4[0:2, 128:256], ab2[0:2, :], r=["ab2a", "ab2b", "oh4"], w=[f"pb{bk}"])
            P.mm(C.pb[bk][:, 128:256], C.oh4[0:1, 0:128], nwr[0:1, :], r=["nwr", "oh4"], w=[f"pb{bk}"])
            for t in range(NT):
                P.act(ealogB[:, t * 8:(t + 1) * 8], C.pb[bk][:, 0:8], AF.Exp, r=[], w=["ealogB", f"pb{bk}"])
                P.copy("dve", dtbB[:, t * 8:(t + 1) * 8], C.pb[bk][:, 8:16], r=[], w=["dtbB", f"pb{bk}"])
            P.copy("dve", nwB[:, :], C.pb[bk][:, 128:256], r=[], w=["nwB", f"pb{bk}"])
        P.mark_const("convT", "nwB", "ealogB", "dtbB")
        DKS = 128.0 ** -0.5
        for s in range(nseq):
            b = s
            row0 = s * S_LEN
            make_gateB(C, 1, b)
            build_hT(C, 1, b, xin, row0, hT, tin)
            P.barrier()
            with ExitStack() as hs:
                wab = P.sb("wab", [128, 128], BF16, hs)
                absb = P.sb("absb", [128, 256], F32, hs)
                tmpg = P.sb("tmpg", [128, 128], F32, hs)
                wh = [P.sb(f"w1h{i}", [128, 4 * 1024], BF16, hs) for i in range(1)]
                uT = P.sb("uT", [128, 2052], BF16, hs)
                sT = P.sb("sT", [128, 2048], BF16, hs)
                k_tm = P.sb("k_tm", [128, 2048], BF16, hs)
                q_tm = P.sb("q_tm", [128, 2048], BF16, hs)
                vb = P.sb("vb", [128, 2048], BF16, hs)
                kbg = P.sb("kbg", [128, 2048], BF16, hs)
                kdec = P.sb("kdec", [128, 2048], BF16, hs)
                tmq = P.sb("tmq", [128, 512], BF16, hs)
                knT = P.sb("knT", [128, 2048], BF16, hs)
                qnT = P.sb("qnT", [128, 2048], BF16, hs)
                qgT = P.sb("qgT", [128, 2048], BF16, hs)
                zs = P.sb("zs", [128, 2048], BF16, hs)
                dgw = P.sb("dgw", [128, 12 * 128], BF16, hs)
                u_sb = P.sb("u_sb", [128, 2048], F32, hs)
                wT_sb = P.sb("wT_sb", [128, 2048], BF16, hs)
                Aq_sb = P.sb("Aq_sb", [128, 2048], BF16, hs)
                sc = P.sb("sc1", [128, 16 * 12], F32, hs)
                junk = P.sb("junk", [128, 128], BF16, hs)
                gtri = [P.sb(f"gtri{i}", [128, 128], F32, hs) for i in range(2)]
                EL = [P.sb(f"EL{i}", [128, 128], F32, hs) for i in range(2)]
                EU = [P.sb(f"EU{i}", [128, 128], F32, hs) for i in range(2)]
                QX = [P.sb(f"QX{i}", [128, 256], F32, hs) for i in range(2)]
                RR = [P.sb(f"RR{i}", [128, 128], F32, hs) for i in range(2)]
                XTb = [P.sb(f"XTb{i}", [128, 128], BF16, hs) for i in range(2)]
                Sst = P.sb("Sst", [128, 128], F32, hs)
                Sbf = P.sb("Sbf", [128, 128], BF16, hs)
                vnew = [P.sb(f"vnew{i}", [128, 128], BF16, hs) for i in range(2)]
                tno = [P.sb(f"tno{i}", [128, 128], F32, hs) for i in range(2)]
                onb = [P.sb(f"on1b{i}", [128, 128], BF16, hs) for i in range(2)]
                P.memset("pool", uT[:, 0:4], 0.0, w=["uT"])
                load_w_bf16(C, W, 0, 4096, 16, wab, 0, "wab")
                bk = next_bank(C)
                for tt in range(NT):
                    for c in range(8):
                        P.mm(C.pb[bk][:, tt * 16:(tt + 1) * 16], hT[:, c * 2048 + tt * 128: c * 2048 + (tt + 1) * 128],
                             wab[:, c * 16:(c + 1) * 16], start=(c == 0), stop=(c == 7), r=[f"hT{c}.{tt // 4}", "wab"], w=[f"pb{bk}"])
                P.copy("dve", absb[:, :], C.pb[bk][:, 0:256], r=[], w=["absb", f"pb{bk}"])
                av = absb[:, :].rearrange("p (t e) -> p t e", t=NT)
                P.tt("dve", tmpg[:, :].rearrange("p (t e) -> p t e", t=NT), av[:, :, 0:8],
                     dtbB[:, :].rearrange("p (t e) -> p t e", t=NT), ALU.add, r=["absb", "dtbB"], w=["tmpg"])
                P.act(tmpg[:, :], tmpg[:, :], AF.Exp, r=[], w=["tmpg"])
                P.act(tmpg[:, :], tmpg[:, :], AF.Ln, bias=1.0, r=[], w=["tmpg"])
                P.tt("dve", gp[:, :], tmpg[:, :], ealogB[:, :], ALU.mult, r=["tmpg", "ealogB"], w=["gp"])
                P.act(beta[:, :].rearrange("p (t e) -> p t e", t=NT), av[:, :, 8:16], AF.Sigmoid, r=["absb"], w=["beta"])
                bk = next_bank(C)
                P.mm(C.pb[bk][:, 0:128], C.triu, gp[:, :], r=["cst", "gp"], w=[f"pb{bk}"])
                P.mm(C.pb[bk][:, 128:256], C.ones, gp[:, :], r=["cst", "gp"], w=[f"pb{bk}"])
                P.copy("dve", Gcol[:, :], C.pb[bk][:, 0:128], r=[], w=["Gcol", f"pb{bk}"])
                P.act(egc[:, :], C.pb[bk][:, 0:128], AF.Exp, scale=-1.0, r=[], w=["egc", f"pb{bk}"])
                P.act(egl[:, :], C.pb[bk][:, 128:256], AF.Exp, scale=-1.0, r=[], w=["egl", f"pb{bk}"])
                P.tt("dve", tmpg[:, :], Gcol[:, :], C.pb[bk][:, 128:256], ALU.subtract, r=["Gcol"], w=["tmpg", f"pb{bk}"])
                P.act(egd[:, :], tmpg[:, :], AF.Exp, r=["tmpg"], w=["egd"])
                betav = beta[:, :].rearrange("p (t e) -> p t e", t=NT)
                egcv = egc[:, :].rearrange("p (t e) -> p t e", t=NT)
                egdv = egd[:, :].rearrange("p (t e) -> p t e", t=NT)
                for h in range(8):
                    whh = wh[0]
                    wk_ = "w1h0"
                    for mi in range(4):
                        load_w_bf16(C, W, 0, mi * 1024 + h * 128, 128, whh, mi * 1024, wk_)
                    P.memset("pool", sc[:, :], 0.0, w=["ssq", "ssk", "sso", "scA", "scB", "scC"])
                    for wi in range(3):
                        for j in range(4):
                            col = (wi * 8 + h) * 4 + j
                            P.ts("dve", dgw[:, (wi * 4 + j) * 128:(wi * 4 + j + 1) * 128], C.identb[:, :], convT[:, col:col + 1], None,
                                 ALU.mult, r=["identb", "convT"], w=["dgw"])
                    for tq in range(4):
                        bk = next_bank(C)
                        for j in range(4):
                            tt = tq * 4 + j
                            for c in range(8):
                                P.mm(C.pb[bk][:, j * 128:(j + 1) * 128], hT[:, c * 2048 + tt * 128: c * 2048 + (tt + 1) * 128],
                                     whh[:, 3 * 1024 + c * 128: 3 * 1024 + (c + 1) * 128], start=(c == 0), stop=(c == 7),
                                     r=[f"hT{c}.{tq}", wk_], w=[f"pb{bk}"])
                        P.act(zs[:, tq * 512:(tq + 1) * 512], C.pb[bk][:, :], AF.Silu, r=[], w=["zs", f"pb{bk}"])
                    for wi in range(3):
                        for tq in range(4):
                            bk = next_bank(C)
                            for c in range(8):
                                P.mm(C.pb[bk][:, :], whh[:, wi * 1024 + c * 128: wi * 1024 + (c + 1) * 128],
                                     hT[:, c * 2048 + tq * 512: c * 2048 + (tq + 1) * 512], start=(c == 0), stop=(c == 7),
                                     r=[wk_, f"hT{c}.{tq}"], w=[f"pb{bk}"])
                            P.copy("act" if tq % 2 else "dve", uT[:, 3 + tq * 512: 3 + (tq + 1) * 512], C.pb[bk][:, :], r=[], w=["uT", f"pb{bk}"])
                        for tq in range(4):
                            bk = next_bank(C)
                            for j in range(4):
                                P.mm(C.pb[bk][:, :], dgw[:, (wi * 4 + j) * 128:(wi * 4 + j + 1) * 128],
                                     uT[:, tq * 512 + j: tq * 512 + j + 512], start=(j == 0), stop=(j == 3), r=["dgw", "uT"], w=[f"pb{bk}"])
                            P.act(sT[:, tq * 512:(tq + 1) * 512], C.pb[bk][:, :], AF.Silu, r=[], w=["sT", f"pb{bk}"])
                        for tp in range(2):
                            for j in range(8):
                                tt = tp * 8 + j
                                P.tr(C.pbT[:, j * 128:(j + 1) * 128], sT[:, tt * 128:(tt + 1) * 128], C.identb[:, :], r=["sT", "identb"], w=["pbT"])
                            for j in range(8):
                                tt = tp * 8 + j
                                src = C.pbT[:, j * 128:(j + 1) * 128]
                                if wi == 0:
                                    P.copy("dve", q_tm[:, tt * 128:(tt + 1) * 128], src, r=[], w=["q_tm", "pbT"])
                                    P.act(junk[:, :], src, AF.Square, accum=sc[:, 0 * 16 + tt: 0 * 16 + tt + 1], r=[], w=["junk", "ssq", "pbT"])
                                elif wi == 1:
                                    P.copy("dve", k_tm[:, tt * 128:(tt + 1) * 128], src, r=[], w=["k_tm", "pbT"])
                                    P.act(junk[:, :], src, AF.Square, accum=sc[:, 1 * 16 + tt: 1 * 16 + tt + 1], r=[], w=["junk", "ssk", "pbT"])
                                else:
                                    P.ts("dve", vb[:, tt * 128:(tt + 1) * 128], src, beta[:, tt * 8 + h: tt * 8 + h + 1], None, ALU.mult,
                                         r=["beta"], w=[f"vb{tt}", "pbT"])
                    P.act(sc[:, 0:32], sc[:, 0:32], AF.Sqrt, bias=1e-6, r=["ssq", "ssk"], w=["scA"])
                    P.op("dve", lambda e, o_=sc[:, 0:32], i_=sc[:, 0:32]: e.reciprocal(o_, i_), r=[], w=["scA"])
                    P.tt("dve", sc[:, 32:48], sc[:, 16:32], betav[:, :, h], ALU.mult, r=["scA", "beta"], w=["scB"])
                    P.tt("dve", sc[:, 32:48], sc[:, 32:48], egcv[:, :, h], ALU.mult, r=["egc"], w=["scB"])
                    P.tt("dve", sc[:, 48:64], sc[:, 16:32], egdv[:, :, h], ALU.mult, r=["scA", "egd"], w=["scB"])
                    P.ts("dve", sc[:, 64:80], sc[:, 0:16], DKS, None, ALU.mult, r=["scA"], w=["scB"])
                    P.tt("dve", sc[:, 80:96], sc[:, 64:80], egcv[:, :, h], ALU.mult, r=["egc"], w=["scB"])
                    P.ts("dve", sc[:, 96:112], betav[:, :, h], -1.0, None, ALU.mult, r=["beta"], w=["scB"])
                    for tt in range(NT):
                        ts_ = slice(tt * 128, (tt + 1) * 128)
                        P.act(kbg[:, ts_], k_tm[:, ts_], AF.Identity, scale=sc[:, 32 + tt:33 + tt], r=["k_tm", "scB"], w=[f"kbg{tt}"])
                        P.ts("dve", kdec[:, ts_], k_tm[:, ts_], sc[:, 48 + tt:49 + tt], None, ALU.mult, r=["k_tm", "scB"], w=[f"kdec{tt}"])
                    for (src_tm, skey, scol, dstT, dkey) in ((k_tm, "k_tm", 16, knT, "knT"), (q_tm, "q_tm", 64, qnT, "qnT"), (q_tm, "q_tm", 80, qgT, "qgT")):
                        for tq in range(4):
                            for j in range(4):
                                tt = tq * 4 + j
                                if j % 2:
                                    P.act(tmq[:, j * 128:(j + 1) * 128], src_tm[:, tt * 128:(tt + 1) * 128], AF.Identity, scale=sc[:, scol + tt: scol + tt + 1],
                                          r=[skey, "scA", "scB"], w=["tmq"])
                                else:
                                    P.ts("dve", tmq[:, j * 128:(j + 1) * 128], src_tm[:, tt * 128:(tt + 1) * 128], sc[:, scol + tt: scol + tt + 1], None,
                                         ALU.mult, r=[skey, "scA", "scB"], w=["tmq"])
                            for j in range(4):
                                P.tr(C.pbT[:, j * 128:(j + 1) * 128], tmq[:, j * 128:(j + 1) * 128], C.identb[:, :], r=["tmq", "identb"], w=["pbT"])
                            P.copy("act" if tq % 2 else "dve", dstT[:, tq * 512:(tq + 1) * 512], C.pbT[:, 0:512], r=[], w=[dkey, "pbT"])
                    def prep(tt):
                        sl = tt % 2
                        col = tt * 8 + h
                        tsl = slice(tt * 128, (tt + 1) * 128)
                        b2 = 2 * sl
                        P.mm(C.pb[b2][:, 0:128], knT[:, tsl], knT[:, tsl], r=["knT"], w=[f"pb{b2}"])
                        P.mm(C.pb[b2][:, 128:256], knT[:, tsl], qnT[:, tsl], r=["knT", "qnT"], w=[f"pb{b2}"])
                        P.ts("dve", gtri[sl][:, :], C.triu, gp[:, col:col + 1], None, ALU.mult, r=["cst", "gp"], w=[f"gtri{sl}"])
                        yield
                        b1 = 2 * sl + 1
                        P.mm(C.pb[b1][:, 0:128], C.ones, gtri[sl][:, :], r=["cst", f"gtri{sl}"], w=[f"pb{b1}"])
                        yield
                        P.stt("dve", EL[sl][:, :], C.pb[b1][:, 0:128], Gcol[:, col:col + 1], C.negL, ALU.subtract, ALU.min,
                              r=["Gcol", "cst"], w=[f"EL{sl}", f"pb{b1}"])
                        P.stt("dve", EU[sl][:, :], C.pb[b1][:, 0:128], Gcol[:, col:col + 1], C.posU, ALU.subtract, ALU.max,
                              r=["Gcol", "cst"], w=[f"EU{sl}", f"pb{b1}"])
                        yield
                        P.act(EL[sl][:, :], EL[sl][:, :], AF.Exp, scale=1.0, r=[], w=[f"EL{sl}"])
                        P.act(EU[sl][:, :], EU[sl][:, :], AF.Exp, scale=-1.0, r=[], w=[f"EU{sl}"])
                        yield
                        P.stt("dve", RR[sl][:, :], C.pb[b2][:, 0:128], sc[:, 96 + tt:97 + tt], EL[sl][:, :], ALU.mult, ALU.mult,
                              r=["scB", f"EL{sl}"], w=[f"RR{sl}", f"pb{b2}"])
                        P.tt("dve", Aq_sb[:, tsl], C.pb[b2][:, 128:256], EU[sl][:, :], ALU.mult, r=[f"EU{sl}"], w=[f"Aq{tt}", f"pb{b2}"])
                        yield
                        b3 = 2 * sl + 1
                        P.tr(C.pb[b3][:, 0:128], RR[sl][:, :], C.ident, r=[f"RR{sl}", "cst"], w=[f"pb{b3}"])
                        yield
                        P.copy("act", QX[sl][:, 0:128], C.pb[b3][:, 0:128], r=[], w=[f"QX{sl}", f"pb{b3}"])
                        P.tt("dve", QX[sl][:, 128:256], C.pb[b3][:, 0:128], C.ident, ALU.add, r=["cst"], w=[f"QX{sl}", f"pb{b3}"])
                        yield
                        for j in range(7):
                            bq = 2 * sl
                            br = 2 * sl + 1
                            if j == 0:
                                P.mm(C.pb[bq][:, 0:128], RR[sl][:, :], QX[sl][:, 0:128], r=[f"RR{sl}", f"QX{sl}"], w=[f"pb{bq}"])
                            elif j <= 4:
                                P.mm(C.pb[bq][:, 0:256], RR[sl][:, :], QX[sl][:, 0:256], r=[f"RR{sl}", f"QX{sl}"], w=[f"pb{bq}"])
                            else:
                                P.mm(C.pb[bq][:, 128:256], RR[sl][:, :], QX[sl][:, 128:256], r=[f"RR{sl}", f"QX{sl}"], w=[f"pb{bq}"])
                            if j <= 5:
                                P.mm(C.pb[br][:, 0:128], QX[sl][:, 0:128], RR[sl][:, :], r=[f"RR{sl}", f"QX{sl}"], w=[f"pb{br}"])
                            yield
                            if j <= 4:
                                P.copy("act", QX[sl][:, 0:128], C.pb[bq][:, 0:128], r=[], w=[f"QX{sl}", f"pb{bq}"])
                            if 1 <= j <= 5:
                                P.tt("dve", QX[sl][:, 128:256], C.pb[bq][:, 128:256], QX[sl][:, 128:256], ALU.add, r=[], w=[f"QX{sl}", f"pb{bq}"])
                            if j == 6:
                                P.tt("dve", XTb[sl][:, :], C.pb[bq][:, 128:256], QX[sl][:, 128:256], ALU.add, r=[f"QX{sl}"], w=[f"XTb{sl}", f"pb{bq}"])
                            if j <= 5:
                                P.copy("dve" if j % 2 else "act", RR[sl][:, :], C.pb[br][:, 0:128], r=[], w=[f"RR{sl}", f"pb{br}"])
                            yield
                        b4 = 2 * sl
                        P.mm(C.pb[b4][:, 0:128], XTb[sl][:, :], vb[:, tsl], r=[f"XTb{sl}", f"vb{tt}"], w=[f"pb{b4}"])
                        P.mm(C.pb[b4][:, 128:256], kbg[:, tsl], XTb[sl][:, :], r=[f"XTb{sl}", f"kbg{tt}"], w=[f"pb{b4}"])
                        yield
                        P.copy("act", u_sb[:, tsl], C.pb[b4][:, 0:128], r=[], w=[f"u{tt}", f"pb{b4}"])
                        P.copy("dve", wT_sb[:, tsl], C.pb[b4][:, 128:256], r=[], w=[f"wT{tt}", f"pb{b4}"])
                        yield

                    def scan(tt):
                        sl = tt % 2
                        col = tt * 8 + h
                        tsl = slice(tt * 128, (tt + 1) * 128)
                        b1 = 4
                        P.mm(C.pb[b1][:, 0:128], wT_sb[:, tsl], Sbf[:, :], r=[f"wT{tt}", "Sbf"], w=[f"pb{b1}"])
                        b2 = 5
                        P.mm(C.pb[b2][:, 0:128], qgT[:, tsl], Sbf[:, :], start=True, stop=False, r=["qgT", "Sbf"], w=[f"pb{b2}"])
                        yield
                        P.tt("dve", vnew[sl][:, :], u_sb[:, tsl], C.pb[b1][:, 0:128], ALU.subtract, r=[f"u{tt}"], w=[f"vnew{sl}", f"pb{b1}"])
                        yield
                        P.mm(C.pb[b2][:, 0:128], Aq_sb[:, tsl], vnew[sl][:, :], start=False, stop=True, r=[f"Aq{tt}", f"vnew{sl}"], w=[f"pb{b2}"])
                        b3 = 6
                        P.mm(C.pb[b3][:, 0:128], kdec[:, tsl], vnew[sl][:, :], r=[f"kdec{tt}", f"vnew{sl}"], w=[f"pb{b3}"])
                        yield
                        P.stt("dve", Sst[:, :], Sst[:, :], egl[:, col:col + 1], C.pb[b3][:, 0:128], ALU.mult, ALU.add, r=["egl"], w=["Sst", f"pb{b3}"])
                        P.act(junk[:, :], C.pb[b2][:, 0:128], AF.Square, accum=sc[:, 112 + tt:113 + tt], r=[], w=["junk", "sso", f"pb{b2}"])
                        yield
                        P.copy("act", Sbf[:, :], Sst[:, :], r=["Sst"], w=["Sbf"])
                        P.copy("dve", u_sb[:, tsl], C.pb[b2][:, 0:128], r=[], w=[f"u{tt}", f"pb{b2}"])
                        yield

                    P.memset("pool", Sst[:, :], 0.0, w=["Sst"])
                    P.memset("pool", Sbf[:, :], 0.0, w=["Sbf"])
                    done = set()

                    def scan_chain():
                        for tt in range(NT):
                            while tt not in done:
                                yield
                            yield from scan(tt)

                    sg = scan_chain()
                    sg_alive = True
                    active = []
                    nxt = 0
                    while sg_alive or active or nxt < NT:
                        while len(active) < 2 and nxt < NT:
                            active.append((nxt, prep(nxt)))
                            nxt += 1
                        for ent in list(active):
                            try:
                                next(ent[1])
                            except StopIteration:
                                done.add(ent[0])
                                active.remove(ent)
                        if sg_alive:
                            try:
                                next(sg)
                            except StopIteration:
                                sg_alive = False
                    P.act(sc[:, 128:144], sc[:, 112:128], AF.Sqrt, bias=1e-6, scale=1.0 / 128.0, r=["sso"], w=["scC"])
                    P.op("dve", lambda e, o_=sc[:, 128:144], i_=sc[:, 128:144]: e.reciprocal(o_, i_), r=[], w=["scC"])
                    for tq in range(4):
                        for j in range(4):
                            tt = tq * 4 + j
                            sl = tt % 2
                            tsl = slice(tt * 128, (tt + 1) * 128)
                            P.stt("dve", tno[sl][:, :], u_sb[:, tsl], sc[:, 128 + tt:129 + tt], nwB[:, :], ALU.mult, ALU.mult,
                                  r=[f"u{tt}", "scC", "nwB"], w=[f"tno{sl}"])
                            P.tt("pool", onb[sl][:, :], tno[sl][:, :], zs[:, tsl], ALU.mult, r=[f"tno{sl}", "zs"], w=[f"on1b{sl}"])
                            P.tr(C.pbT[:, j * 128:(j + 1) * 128], onb[sl][:, :], C.identb[:, :], r=[f"on1b{sl}", "identb"], w=["pbT"])
                        P.copy("act" if tq % 2 else "dve", oT[:, h * 2048 + tq * 512: h * 2048 + (tq + 1) * 512], C.pbT[:, 0:512],
                               r=[], w=[f"oT{h}", "pbT"])
            P.barrier()
            with ExitStack() as os_:
                wbig = P.sb("wbig1", [128, 8 * 1024], BF16, os_)
                load_w_bf16(C, D["gdn_w_out"], 0, 0, 1024, wbig, 0, "wbig")
                out_proj_ln(C, 1, b, oT, wbig, xin, xout, row0, tin, tout)
            P.barrier()


def make_consts():
    cst = np.zeros((128, 768), np.float32)
    cst[:, 0:128] = np.eye(128)
    cst[:, 128:256] = np.triu(np.ones((128, 128)))
    cst[:, 256:384] = np.tril(np.ones((128, 128)), -1)
    cst[:, 384:512] = 1.0
    cst[:, 512:640] = np.where(np.tril(np.ones((128, 128)), -1) > 0, 0.0, -1.0e4)
    cst[:, 640:768] = np.where(np.triu(np.ones((128, 128))) > 0, 0.0, 1.0e4)
    oh4 = np.kron(np.eye(4, dtype=np.float32), np.ones((1, 128), np.float32))
    oh12 = np.kron(np.eye(12, dtype=np.float32), np.ones((1, 4), np.float32))
    return cst, oh4, oh12


def build(nseq, mode):
    nc = bass.Bass("TRN2", target_bir_lowering=False)
    D = {}

    def din(name, shape):
        D[name] = nc.dram_tensor(name, list(shape), F32, kind="ExternalInput").ap()

    din("x", [nseq * S_LEN, DM])
    din("c4", [4, DM])
    din("ada_w", [2 * DM, 3 * DM])
    din("ada_b", [12, 512])
    din("oh12", [12, 48])
    din("ln_g", [2, DM])
    din("ln_b", [2, DM])
    din("moba_w_in", [DM, 4 * DM])
    din("moba_w_out", [DM, DM])
    din("gdn_w_in", [DM, 4112])
    din("gdn_conv_w", [4, 3072])
    din("gdn_a_log", [1, 8])
    din("gdn_dt_bias", [1, 8])
    din("gdn_norm_w", [1, 128])
    din("gdn_w_out", [DM, DM])
    din("cst", [128, 768])
    din("oh4", [4, 512])
    out = nc.dram_tensor("out", [nseq * S_LEN, DM], F32, kind="ExternalOutput").ap()
    with ExitStack() as st:
        P = Prog(nc, st)
        C = setup_common(P, D)
        P.barrier()
        if mode == "l0":
            layer0(C, nseq, D["x"], out)
        elif mode == "l1":
            layer1(C, nseq, D["x"], out)
        else:
            x1 = nc.dram_tensor("x1s", [nseq * S_LEN, DM], F32).ap()
            layer0(C, nseq, D["x"], x1, "x", "x1")
            P.barrier()
            layer1(C, nseq, x1, out, "x1", "out")
        stats = P.emit()
    return nc, stats


NCORES = 8
NSEQ = 4


def kernel(**inputs):
    f = lambda k: np.ascontiguousarray(np.asarray(inputs[k], dtype=np.float32))
    x = f("x")
    c = f("c")
    B = x.shape[0]
    assert B == NCORES * NSEQ and x.shape[1] == S_LEN and x.shape[2] == DM
    nc, _ = build(NSEQ, "full")
    cst, oh4, oh12 = make_consts()
    shared = {
        "ada_w": f("ada_w").reshape(2 * DM, 3 * DM),
        "ada_b": f("ada_b").reshape(12, 512),
        "oh12": oh12,
        "ln_g": f("ln_g"),
        "ln_b": f("ln_b"),
        "moba_w_in": f("moba_w_in")[0],
        "moba_w_out": f("moba_w_out")[0],
        "gdn_w_in": f("gdn_w_in")[0],
        "gdn_conv_w": f("gdn_conv_w")[0],
        "gdn_a_log": f("gdn_a_log"),
        "gdn_dt_bias": f("gdn_dt_bias"),
        "gdn_norm_w": f("gdn_norm_w"),
        "gdn_w_out": f("gdn_w_out")[0],
        "cst": cst,
        "oh4": oh4,
    }
    in_maps = []
    for core in range(NCORES):
        b0 = core * NSEQ
        m = dict(shared)
        m["x"] = x[b0:b0 + NSEQ].reshape(NSEQ * S_LEN, DM)
        m["c4"] = c[b0:b0 + NSEQ]
        in_maps.append(m)
    res = run_bass_kernel_spmd(nc, in_maps, core_ids=list(range(NCORES)))
    out = np.concatenate([np.asarray(r["out"]).reshape(NSEQ, S_LEN, DM) for r in res.results], axis=0)
    return out.astype(np.float32)
```
